# Optimizing a Trainium2 kernel written in Bass

```python
import jax, jax.numpy as jnp
from jax import lax
import numpy as np

D_MODEL = 1024
BATCH = 16
SEQ = 2048
DEPTH = 2

CTX_LEN = 256
GRID_W = 64
FOURIER_WIDTH = 512
FOURIER_GROUPS = 4
LRU_WIDTH = 512
LRU_HEADS = 8
LRU_HEAD_DIM = LRU_WIDTH // LRU_HEADS
LRU_CONV = 4
LRU_C = 8.0
CONF_WIDTH = 512
CONF_KERNEL = 31
FFN_HIDDEN = 2816
FFN_CONV = 3
N_BRANCH = 3
EPS = 1e-6
LN_EPS = 1e-5

OFF_F = 0
OFF_LX = OFF_F + FOURIER_WIDTH
OFF_LG = OFF_LX + LRU_WIDTH
OFF_C = OFF_LG + LRU_WIDTH
OFF_G = OFF_C + 2 * CONF_WIDTH
IN_WIDTH = OFF_G + N_BRANCH * D_MODEL

kernel_name = "hybrid_fourier_rglru_conformer_dit"


def rms_norm(x, g):
    xf = x.astype(jnp.float32)
    y = xf * lax.rsqrt(jnp.mean(xf * xf, axis=-1, keepdims=True) + EPS)
    return (y * g.astype(jnp.float32)).astype(x.dtype)


def layer_norm(x, g, b):
    xf = x.astype(jnp.float32)
    mu = jnp.mean(xf, axis=-1, keepdims=True)
    var = jnp.mean(jnp.square(xf - mu), axis=-1, keepdims=True)
    y = (xf - mu) * lax.rsqrt(var + LN_EPS)
    return (y * g.astype(jnp.float32) + b.astype(jnp.float32)).astype(x.dtype)


def modulate(h, shift, scale):
    return h * (1.0 + scale) + shift


def dwconv(x, w, b, left, rows):
    n, L, C = x.shape
    K = w.shape[0]
    if rows is not None:
        x = x.reshape(n * rows, L // rows, C)
    y = lax.conv_general_dilated(x, w[:, None, :], window_strides=(1,),
                                 padding=[(left, K - 1 - left)],
                                 dimension_numbers=('NWC', 'WIO', 'NWC'),
                                 feature_group_count=C)
    return y.reshape(n, L, C) + b


def fourier_mix(u):
    n, L, _ = u.shape
    z = u.astype(jnp.float32).reshape(n, L, FOURIER_GROUPS, FOURIER_WIDTH // FOURIER_GROUPS)
    z = jnp.fft.fft2(z, axes=(1, 3), norm="ortho").real
    return z.reshape(n, L, FOURIER_WIDTH).astype(u.dtype)


def linear_scan(a, b, h0, reverse):
    if h0 is not None:
        if reverse:
            b = b.at[:, -1].add(a[:, -1] * h0)
        else:
            b = b.at[:, 0].add(a[:, 0] * h0)

    def comb(l, r):
        a_l, b_l = l
        a_r, b_r = r
        return a_l * a_r, a_r * b_l + b_r

    _, h = lax.associative_scan(comb, (a, b), reverse=reverse, axis=1)
    return h


def rglru_scan(u, p, rows, h0f, h0b):
    xc = dwconv(u, p['lru_conv_w'], p['lru_conv_b'], LRU_CONV // 2, rows)
    n, L, _ = xc.shape
    xf = xc.astype(jnp.float32)
    xh = xf.reshape(n, L, LRU_HEADS, LRU_HEAD_DIM)

    def direction(d, h0, reverse):
        r = jax.nn.sigmoid(jnp.einsum('blhi,hij->blhj', xh, p['lru_wa'][d].astype(jnp.float32)).reshape(n, L, LRU_WIDTH)
                           + p['lru_ba'][d].astype(jnp.float32))
        i = jax.nn.sigmoid(jnp.einsum('blhi,hij->blhj', xh, p['lru_wx'][d].astype(jnp.float32)).reshape(n, L, LRU_WIDTH)
                           + p['lru_bx'][d].astype(jnp.float32))
        log_a = -LRU_C * r * jax.nn.softplus(-p['lru_lam'][d].astype(jnp.float32))
        a = jnp.exp(log_a)
        b = jnp.sqrt(-jnp.expm1(2.0 * log_a)) * (i * xf)
        return linear_scan(a, b, h0, reverse)

    hf = direction(0, h0f, False)
    hb = direction(1, h0b, True)
    return hf, hb


def token_mixer(proj, p, rows, h0f, h0b):
    n, L, _ = proj.shape
    u_f = proj[..., OFF_F:OFF_LX]
    u_x = proj[..., OFF_LX:OFF_LG]
    u_g = proj[..., OFF_LG:OFF_C]
    u_c = proj[..., OFF_C:OFF_G]
    gates = jax.nn.sigmoid(proj[..., OFF_G:]).reshape(n, L, N_BRANCH, D_MODEL)

    y_f = fourier_mix(u_f) @ p['fourier_out_w']

    hf, hb = rglru_scan(u_x, p, rows, h0f, h0b)
    y_r = ((hf + hb) * jax.nn.gelu(u_g.astype(jnp.float32))).astype(proj.dtype) @ p['lru_out_w']

    a, g = jnp.split(u_c, 2, axis=-1)
    v = a * jax.nn.sigmoid(g)
    v = dwconv(v, p['conf_conv_w'], p['conf_conv_b'], CONF_KERNEL // 2, rows)
    v = jax.nn.silu(layer_norm(v, p['conf_ln_g'], p['conf_ln_b']))
    y_c = v @ p['conf_out_w']

    merged = gates[..., 0, :] * y_f + gates[..., 1, :] * y_r + gates[..., 2, :] * y_c
    return merged @ p['mix_out_w'], hf, hb


def conv_ffn(h, p, rows):
    u = h @ p['ffn_up_w']
    u = dwconv(u, p['ffn_conv_w'], p['ffn_conv_b'], FFN_CONV // 2, rows)
    v, g = jnp.split(u, 2, axis=-1)
    return (jax.nn.silu(g) * v) @ p['ffn_down_w']


def setup_inputs(seed: int = 0) -> dict:
    key = jax.random.key(seed)
    ks = iter(jax.random.split(key, 40))
    f32 = jnp.float32

    def nrm(shape, scale):
        return jax.random.normal(next(ks), shape, f32) * scale

    L = DEPTH
    u = jax.random.uniform(next(ks), (L, 2, LRU_WIDTH), f32, minval=0.9, maxval=0.999)
    a0 = u ** (1.0 / LRU_C)
    lam = jnp.log(a0) - jnp.log1p(-a0)
    return {
        "x": nrm((BATCH, SEQ, D_MODEL), 1.0),
        "c": nrm((BATCH, D_MODEL), 1.0),
        "ctx": nrm((BATCH, CTX_LEN, D_MODEL), 1.0),
        "c_ctx": nrm((D_MODEL,), 1.0),
        "ada_w": nrm((L, D_MODEL, 6 * D_MODEL), D_MODEL ** -0.5),
        "ada_b": nrm((L, 6 * D_MODEL), 0.02),
        "norm1_g": 1.0 + nrm((L, D_MODEL), 0.05),
        "norm2_g": 1.0 + nrm((L, D_MODEL), 0.05),
        "in_w": nrm((L, D_MODEL, IN_WIDTH), D_MODEL ** -0.5),
        "in_b": nrm((L, IN_WIDTH), 0.02),
        "fourier_out_w": nrm((L, FOURIER_WIDTH, D_MODEL), FOURIER_WIDTH ** -0.5),
        "lru_conv_w": nrm((L, LRU_CONV, LRU_WIDTH), LRU_CONV ** -0.5),
        "lru_conv_b": nrm((L, LRU_WIDTH), 0.02),
        "lru_wa": nrm((L, 2, LRU_HEADS, LRU_HEAD_DIM, LRU_HEAD_DIM), LRU_HEAD_DIM ** -0.5),
        "lru_ba": nrm((L, 2, LRU_WIDTH), 0.02),
        "lru_wx": nrm((L, 2, LRU_HEADS, LRU_HEAD_DIM, LRU_HEAD_DIM), LRU_HEAD_DIM ** -0.5),
        "lru_bx": nrm((L, 2, LRU_WIDTH), 0.02),
        "lru_lam": lam,
        "lru_out_w": nrm((L, LRU_WIDTH, D_MODEL), LRU_WIDTH ** -0.5),
        "conf_conv_w": nrm((L, CONF_KERNEL, CONF_WIDTH), CONF_KERNEL ** -0.5),
        "conf_conv_b": nrm((L, CONF_WIDTH), 0.02),
        "conf_ln_g": 1.0 + nrm((L, CONF_WIDTH), 0.05),
        "conf_ln_b": nrm((L, CONF_WIDTH), 0.02),
        "conf_out_w": nrm((L, CONF_WIDTH, D_MODEL), CONF_WIDTH ** -0.5),
        "mix_out_w": nrm((L, D_MODEL, D_MODEL), D_MODEL ** -0.5),
        "ffn_up_w": nrm((L, D_MODEL, 2 * FFN_HIDDEN), D_MODEL ** -0.5),
        "ffn_conv_w": nrm((L, FFN_CONV, 2 * FFN_HIDDEN), FFN_CONV ** -0.5),
        "ffn_conv_b": nrm((L, 2 * FFN_HIDDEN), 0.02),
        "ffn_down_w": nrm((L, FFN_HIDDEN, D_MODEL), FFN_HIDDEN ** -0.5),
        "final_g": 1.0 + nrm((D_MODEL,), 0.05),
    }


def reference(x, c, ctx, c_ctx, ada_w, ada_b, norm1_g, norm2_g, in_w, in_b, fourier_out_w,
              lru_conv_w, lru_conv_b, lru_wa, lru_ba, lru_wx, lru_bx, lru_lam, lru_out_w,
              conf_conv_w, conf_conv_b, conf_ln_g, conf_ln_b, conf_out_w, mix_out_w,
              ffn_up_w, ffn_conv_w, ffn_conv_b, ffn_down_w, final_g):
    rows = x.shape[1] // GRID_W
    h_lat = x
    h_ctx = ctx
    for i in range(DEPTH):
        p = dict(fourier_out_w=fourier_out_w[i], lru_conv_w=lru_conv_w[i], lru_conv_b=lru_conv_b[i],
                 lru_wa=lru_wa[i], lru_ba=lru_ba[i], lru_wx=lru_wx[i], lru_bx=lru_bx[i],
                 lru_lam=lru_lam[i], lru_out_w=lru_out_w[i], conf_conv_w=conf_conv_w[i],
                 conf_conv_b=conf_conv_b[i], conf_ln_g=conf_ln_g[i], conf_ln_b=conf_ln_b[i],
                 conf_out_w=conf_out_w[i], mix_out_w=mix_out_w[i], ffn_up_w=ffn_up_w[i],
                 ffn_conv_w=ffn_conv_w[i], ffn_conv_b=ffn_conv_b[i], ffn_down_w=ffn_down_w[i])
        last = i == DEPTH - 1

        mod_lat = (jax.nn.silu(c) @ ada_w[i] + ada_b[i])[:, None, :]
        mod_ctx = jax.nn.silu(c_ctx) @ ada_w[i] + ada_b[i]
        sh1, sc1, g1, sh2, sc2, g2 = jnp.split(mod_lat, 6, axis=-1)
        csh1, csc1, cg1, csh2, csc2, cg2 = jnp.split(mod_ctx, 6, axis=-1)

        n_ctx = modulate(rms_norm(h_ctx, norm1_g[i]), csh1, csc1)
        if last:
            u_ctx = n_ctx @ in_w[i][:, OFF_LX:OFF_LG] + in_b[i][OFF_LX:OFF_LG]
            hf_c, hb_c = rglru_scan(u_ctx, p, None, None, None)
        else:
            proj_c = n_ctx @ in_w[i] + in_b[i]
            mix_c, hf_c, hb_c = token_mixer(proj_c, p, None, None, None)
            h_ctx = h_ctx + cg1 * mix_c
            h_ctx = h_ctx + cg2 * conv_ffn(modulate(rms_norm(h_ctx, norm2_g[i]), csh2, csc2), p, None)

        n_lat = modulate(rms_norm(h_lat, norm1_g[i]), sh1, sc1)
        proj = n_lat @ in_w[i] + in_b[i]
        mix, _, _ = token_mixer(proj, p, rows, hf_c[:, -1], hb_c[:, 0])
        h_lat = h_lat + g1 * mix
        h_lat = h_lat + g2 * conv_ffn(modulate(rms_norm(h_lat, norm2_g[i]), sh2, sc2), p, rows)
    return rms_norm(h_lat, final_g)
```

```python
import numpy as np
from contextlib import ExitStack
import concourse.bass as bass
import concourse.mybir as mybir
from concourse.bass_utils import run_bass_kernel_spmd

F32 = mybir.dt.float32
BF16 = mybir.dt.bfloat16
AF = mybir.ActivationFunctionType
ALU = mybir.AluOpType

D = 1024
KC = 8
SEQ = 2048
CTX = 256
T = CTX + SEQ
DEPTH = 2
FH = 2816
OFF_LX, OFF_LG, OFF_C, OFF_G = 512, 1024, 1536, 2560
IN_W = 5632
NCORES = 8
BPC = 2

VEC = {}
_off = 0
for _n, _w in (("ada_b", 48), ("n1g", 8), ("n2g", 8), ("in_b", 44), ("lcw", 16), ("lcb", 4),
               ("lba", 8), ("lbx", 8), ("lam", 8), ("ccw", 124), ("ccb", 4), ("clg", 4),
               ("clb", 4), ("fcw", 132), ("fcb", 44), ("fing", 8)):
    VEC[_n] = _off
    _off += _w
NV = _off

SEG512 = [(0, 256, True)] + [(256 + 512 * i, 512, False) for i in range(4)]
SEG256 = [(0, 256, True)] + [(256 + 256 * i, 256, False) for i in range(8)]


class Reg:
    __slots__ = ("w", "r")

    def __init__(self):
        self.w = {}
        self.r = {}


class KB:
    def __init__(self, nc, es):
        self.nc = nc
        self.eng = {"pe": nc.tensor, "act": nc.scalar, "dve": nc.vector, "pool": nc.gpsimd, "sp": nc.sync}
        self.sem = {}
        self.cnt = {}
        self.waited = {e: {} for e in self.eng}
        for e in self.eng:
            self.sem[e] = es.enter_context(nc.semaphore("s_" + e))
            self.cnt[e] = 0
        self.R = 8
        for q in ("sp", "pool"):
            for i in range(self.R):
                k = ("dq", q, i)
                self.sem[k] = es.enter_context(nc.semaphore("d_%s%d" % (q, i)))
                self.cnt[k] = 0
        self.rr = {"sp": 0, "pool": 0}
        self.regs = {}
        self.nins = 0
        self.ghost = Reg()

    def reg(self, *key):
        r = self.regs.get(key)
        if r is None:
            r = self.regs[key] = Reg()
            r.w = dict(self.ghost.w)
            r.r = dict(self.ghost.r)
        return r

    def phase_end(self, keep=("H", "NT", "ps", "consts", "YF", "YR", "YC", "dftLb")):
        g = self.ghost
        for key, r in list(self.regs.items()):
            for k, v in r.w.items():
                if g.w.get(k, 0) < v:
                    g.w[k] = v
            for k, v in r.r.items():
                if g.r.get(k, 0) < v:
                    g.r[k] = v
            if key[0] not in keep:
                del self.regs[key]

    def _need(self, E, k, v, waits):
        if self.waited[E].get(k, 0) >= v:
            return
        if waits.get(k, 0) < v:
            waits[k] = v

    def _emit_waits(self, E, waits):
        for k, v in waits.items():
            self.eng[E].wait_ge(self.sem[k], v)
            self.waited[E][k] = v
            self.nins += 1

    def _deps(self, E, reads, writes, ident, waits):
        for r in reads:
            for k, v in r.w.items():
                if k == ident and E == "pe":
                    continue
                self._need(E, k, v, waits)
        for w in writes:
            for k, v in w.w.items():
                if k == ident and E == "pe":
                    continue
                self._need(E, k, v, waits)
            for k, v in w.r.items():
                if k == ident:
                    continue
                self._need(E, k, v, waits)

    def op(self, E, fn, reads=(), writes=()):
        waits = {}
        self._deps(E, reads, writes, E, waits)
        self._emit_waits(E, waits)
        ins = fn(self.eng[E])
        self.cnt[E] += 1
        c = self.cnt[E]
        ins.then_inc(self.sem[E], 1)
        self.nins += 1
        for r in reads:
            r.r[E] = c
        for w in writes:
            w.w = {E: c}
            w.r = {}

    def dma(self, Q, out, in_, reads=(), writes=()):
        i = self.rr[Q] % self.R
        self.rr[Q] += 1
        k = ("dq", Q, i)
        waits = {}
        if self.cnt[k] > 0:
            self._need(Q, k, self.cnt[k], waits)
        self._deps(Q, reads, writes, None, waits)
        self._emit_waits(Q, waits)
        ins = self.eng[Q].dma_start(out=out, in_=in_)
        self.cnt[k] += 16
        c = self.cnt[k]
        ins.then_inc(self.sem[k], 16)
        self.nins += 1
        for r in reads:
            r.r[k] = c
        for w in writes:
            w.w = {k: c}
            w.r = {}

    def barrier(self):
        for E in self.eng:
            waits = {}
            for k, c in self.cnt.items():
                if k != E and c > 0:
                    self._need(E, k, c, waits)
            self._emit_waits(E, waits)
        self.regs.clear()


class Rot:
    uid = 0

    def __init__(self, kb, es, name, shape, dtype, n):
        self.kb = kb
        self.name = name
        Rot.uid += 1
        self.t = [es.enter_context(kb.nc.sbuf_tensor("%s_r%d_%d" % (name, Rot.uid, i), shape, dtype)) for i in range(n)]
        self.i = 0

    def get(self):
        i = self.i % len(self.t)
        self.i += 1
        return self.t[i], self.kb.reg(self.name, i)


class ViewRot:
    def __init__(self, kb, name, views):
        self.kb = kb
        self.name = name
        self.t = list(views)
        self.i = 0

    def get(self):
        i = self.i % len(self.t)
        self.i += 1
        return self.t[i], self.kb.reg(self.name, i)


def build_program():
    nc = bass.Bass("TRN2", target_bir_lowering=False)
    dt = nc.dram_tensor
    xT = dt("xT", [BPC, D, SEQ], F32, kind="ExternalInput").ap()
    ctxT = dt("ctxT", [BPC, D, CTX], F32, kind="ExternalInput").ap()
    cT = dt("cT", [128, KC, 3], F32, kind="ExternalInput").ap()
    vecs_d = dt("vecs", [128, DEPTH, NV], F32, kind="ExternalInput").ap()
    inbf_d = dt("inbf", [DEPTH, 1, 512], F32, kind="ExternalInput").ap()
    ada_w = dt("ada_w", [DEPTH, D, 6 * D], F32, kind="ExternalInput").ap()
    in_w = dt("in_w", [DEPTH, D, IN_W], F32, kind="ExternalInput").ap()
    fo_w = dt("fourier_out_w", [DEPTH, 512, D], F32, kind="ExternalInput").ap()
    lo_w = dt("lru_out_w", [DEPTH, 512, D], F32, kind="ExternalInput").ap()
    co_w = dt("conf_out_w", [DEPTH, 512, D], F32, kind="ExternalInput").ap()
    mix_w = dt("mix_out_w", [DEPTH, D, D], F32, kind="ExternalInput").ap()
    up_w = dt("ffn_up_w", [DEPTH, D, 2 * FH], F32, kind="ExternalInput").ap()
    dn_w = dt("ffn_down_w", [DEPTH, FH, D], F32, kind="ExternalInput").ap()
    lwa = dt("lru_wa", [DEPTH, 2, 8, 64, 64], F32, kind="ExternalInput").ap()
    lwx = dt("lru_wx", [DEPTH, 2, 8, 64, 64], F32, kind="ExternalInput").ap()
    dftL = dt("dftL", [2, 4, 128, 16 * 256], F32, kind="ExternalInput").ap()
    dftM = dt("dftM", [2, 128, 32], F32, kind="ExternalInput").ap()
    dftC = dt("dftC", [2, 128, 2 * 256], F32, kind="ExternalInput").ap()
    dftK = dt("dftK", [3, 128, 128], F32, kind="ExternalInput").ap()
    ident_d = dt("ident", [128, 128], F32, kind="ExternalInput").ap()
    outT = dt("outT", [BPC, D, SEQ], F32, kind="ExternalOutput").ap()
    dftLb = dt("dftLb", [2, 4, 128, 16 * 256], BF16, kind="Internal").ap()

    with ExitStack() as es:
        es.enter_context(nc.allow_low_precision("bf16 matmul operands, fp32 accumulation"))
        kb = KB(nc, es)
        uid = [0]

        def sb(name, shape, dtype, st=es):
            uid[0] += 1
            return st.enter_context(nc.sbuf_tensor("%s_u%d" % (name, uid[0]), shape, dtype))

        NT = sb("NT", [128, KC, T], BF16)
        BREG = sb("BREG", [128, 3 * 4 * T], BF16)
        YF = BREG[:, 0:4 * T].rearrange("p (c t) -> p c t", c=4)
        YR = BREG[:, 4 * T:8 * T].rearrange("p (c t) -> p c t", c=4)
        YC = BREG[:, 8 * T:12 * T].rearrange("p (c t) -> p c t", c=4)
        hsp = dt("hsp", [KC, 128, T], F32, kind="Internal").ap()
        KEEP = ("NT", "ps", "consts", "consts2", "dftLb", "hsp")
        VECS = sb("VECS", [128, DEPTH, NV], F32)
        MOD = sb("MOD", [128, DEPTH, 48, 3], F32)
        GS = sb("GS", [128, DEPTH, 2, KC, 3], F32)
        CONST = sb("CONST", [128, 4], F32)
        IDB = sb("IDB", [128, 128], BF16)
        ONES = sb("ONES", [128, 128], BF16)
        CKS = sb("CKS", [128, 3, 128], BF16)
        SCT = sb("SCT", [128, KC, 3], BF16)
        PS = [es.enter_context(nc.psum_tensor("ps%d" % i, [128, 512], F32)) for i in range(8)]
        bank_i = [0]

        def bank():
            i = bank_i[0] % 8
            bank_i[0] += 1
            return PS[i], kb.reg("ps", i)

        rH = lambda c, s: kb.reg("H", c, s)
        rNT = lambda c, s: kb.reg("NT", c, s)
        rC = kb.reg("consts")

        def vec(l, name, idx=0):
            o = VEC[name] + idx
            return VECS[:, l, o:o + 1]

        def mod(l, which, kc, r):
            return MOD[:, l, which * 8 + kc, r:r + 1]

        evac_flip = [0]

        def copy_any(out, in_, reads, writes):
            evac_flip[0] ^= 1
            if evac_flip[0]:
                kb.op("act", lambda e: e.activation(out=out, in_=in_, func=AF.Identity), reads, writes)
            else:
                kb.op("dve", lambda e: e.tensor_copy(out=out, in_=in_), reads, writes)

        def mm(ps_ap, lhsT, rhs, start, stop, reads, writes):
            kb.op("pe", lambda e: e.matmul(ps_ap, lhsT, rhs, start=start, stop=stop), reads, writes)

        kb.dma("sp", VECS[:], vecs_d, (), (rC,))
        kb.dma("pool", IDB[:], ident_d, (), (rC,))
        kb.dma("pool", CKS[:, 0, :], dftK[0], (), (rC,))
        kb.dma("pool", CKS[:, 1, :], dftK[1], (), (rC,))
        kb.dma("pool", CKS[:, 2, :], dftK[2], (), (rC,))
        kb.op("dve", lambda e: e.memset(ONES[:], 1.0), (), (rC,))
        kb.op("dve", lambda e: e.memset(CONST[:, 0:1], 1e-6), (), (rC,))
        kb.op("dve", lambda e: e.memset(CONST[:, 1:2], 1e-5), (), (rC,))
        kb.op("dve", lambda e: e.memset(CONST[:, 2:3], 1.0), (), (rC,))
        kb.op("dve", lambda e: e.memset(CONST[:, 3:4], 0.0), (), (rC,))
        EPS_RMS, EPS_LN, ONE1 = CONST[:, 0:1], CONST[:, 1:2], CONST[:, 2:3]
        rdl = kb.reg("dftLb")
        rC2 = kb.reg("consts2")

        def ada_gen(l, WAr, rcm, bankfn):
            aw = ada_w[l].rearrange("(kc p) n -> p kc n", p=128)

            def load(cg_):
                wt_, wr_ = WAr.get()
                kb.dma("pool", wt_[:], aw[:, :, cg_ * 512:(cg_ + 1) * 512], (), (wr_,))
                return wt_, wr_

            nxt_ = load(0)
            for cg in range(12):
                wt, wr = nxt_
                if cg < 11:
                    nxt_ = load(cg + 1)
                ps, pr = bankfn()
                for f in range(4):
                    for kc in range(KC):
                        mm(ps[:, f * 3:(f + 1) * 3], wt[:, kc, f * 128:(f + 1) * 128], SCT[:, kc, :],
                           kc == 0, kc == KC - 1, (wr, rC), (pr,))
                for r in range(3):
                    o = VEC["ada_b"] + cg * 4
                    kb.op("dve", lambda e, r=r, ps=ps, o=o, cg=cg: e.tensor_tensor(
                        out=MOD[:, l, cg * 4:(cg + 1) * 4, r],
                        in0=ps[:, 0:12].rearrange("p (f r) -> p f r", r=3)[:, :, r],
                        in1=VECS[:, l, o:o + 4], op=ALU.add), (pr, rC), (rcm,))
                yield
            for n_i, (gname, which) in enumerate((("n1g", 1), ("n2g", 4))):
                for r in range(3):
                    kb.op("dve", lambda e, n_i=n_i, which=which, r=r: e.tensor_scalar(
                        out=GS[:, l, n_i, :, r], in0=MOD[:, l, which * 8:(which + 1) * 8, r],
                        scalar1=1.0, scalar2=None, op0=ALU.add), (rcm,), (rcm,))
                    o = VEC[gname]
                    kb.op("dve", lambda e, n_i=n_i, r=r, o=o: e.tensor_tensor(
                        out=GS[:, l, n_i, :, r], in0=GS[:, l, n_i, :, r], in1=VECS[:, l, o:o + 8],
                        op=ALU.mult), (rcm, rC), (rcm,))

        with ExitStack() as p0:
            CTs = sb("CTs", [128, KC, 3], F32, p0)
            rct = kb.reg("CTs")
            kb.dma("sp", CTs[:], cT, (), (rct,))
            kb.op("act", lambda e: e.activation(out=SCT[:], in_=CTs[:], func=AF.Silu), (rct,), (rC,))
            WA = Rot(kb, p0, "adaW", [128, KC, 512], BF16, 2)
            DS = Rot(kb, p0, "dstage", [128, 4096], F32, 3)
            DB = Rot(kb, p0, "dstageb", [128, 4096], BF16, 2)
            chunks = [(half, mb) for mb in range(4) for half in range(2)]
            staged = []
            for ci in range(len(chunks) + 2):
                if ci < len(chunks):
                    ds, dsr = DS.get()
                    kb.dma("sp", ds[:], dftL[chunks[ci][0], chunks[ci][1]], (), (dsr,))
                    staged.append((ds, dsr))
                if ci >= 2:
                    half, mb = chunks[ci - 2]
                    ds, dsr = staged[ci - 2]
                    db, dbr = DB.get()
                    copy_any(db[:], ds[:], (dsr,), (dbr,))
                    kb.dma("sp", dftLb[half, mb], db[:], (dbr,), (rdl,))
            for _ in ada_gen(0, WA, rC, bank):
                pass
            kb.barrier()

        def run_rr(gens):
            gens = list(gens)
            while gens:
                for g_ in list(gens):
                    try:
                        next(g_)
                    except StopIteration:
                        gens.remove(g_)

        def rr_gen(gens):
            gens = list(gens)
            while gens:
                for g_ in list(gens):
                    try:
                        next(g_)
                    except StopIteration:
                        gens.remove(g_)
                yield

        bankL_i = [0]
        bankC_i = [0]

        def bankL():
            i = bankL_i[0] % 4
            bankL_i[0] += 1
            return PS[i], kb.reg("ps", i)

        def bankC():
            i = 4 + bankC_i[0] % 4
            bankC_i[0] += 1
            return PS[i], kb.reg("ps", i)

        def rms_mod(l, n_i, sh_which, bi, segs, tmp, out_fn, extra=(), nslots=2):
            SQ, RS, TM = tmp

            def blocks(slot):
                sq, sqr = SQ.t[slot], kb.reg(SQ.name, slot)
                rs, rsr = RS.t[slot], kb.reg(RS.name, slot)
                tm, tmr = TM.t[slot], kb.reg(TM.name, slot)
                for (s, N, isctx) in segs[slot::nslots]:
                    si = 0 if isctx else 1 + (s - CTX) // 512
                    r = 2 if isctx else bi
                    ps, pr = PS[8 - nslots + slot], kb.reg("ps", 8 - nslots + slot)
                    for kc in range(KC):
                        kb.op("act", lambda e: e.activation(
                            out=sq[:, 0:N], in_=H[:, kc, s:s + N], func=AF.Square), (rH(kc, si),), (sqr,))
                        mm(ps[:, 0:N], ONES[:], sq[:, 0:N], kc == 0, kc == KC - 1, (sqr, rC), (pr,))
                        yield
                    kb.op("act", lambda e: e.activation(
                        out=rs[:, 0:N], in_=ps[:, 0:N], func=AF.Ln, scale=1.0 / D, bias=EPS_RMS), (pr, rC), (rsr,))
                    yield
                    kb.op("act", lambda e: e.activation(
                        out=rs[:, 0:N], in_=rs[:, 0:N], func=AF.Exp, scale=-0.5), (rsr,), (rsr,))
                    yield
                    for kc in range(KC):
                        tmk, tmkr = tm, tmr
                        kb.op("dve", lambda e: e.tensor_tensor(
                            out=tmk[:, 0:N], in0=H[:, kc, s:s + N], in1=rs[:, 0:N], op=ALU.mult),
                            (rH(kc, si), rsr), (tmkr,))
                        yield
                        o_ap, o_r = out_fn(kc, si, s, N)
                        if kc % 4 == 3:
                            kb.op("dve", lambda e: e.tensor_scalar(
                                out=o_ap, in0=tmk[:, 0:N], scalar1=GS[:, l, n_i, kc, r:r + 1],
                                scalar2=mod(l, sh_which, kc, r), op0=ALU.mult, op1=ALU.add), (tmkr, rC, rC2), (o_r,))
                        else:
                            kb.op("act", lambda e: e.activation(
                                out=o_ap, in_=tmk[:, 0:N], func=AF.Identity, scale=GS[:, l, n_i, kc, r:r + 1],
                                bias=mod(l, sh_which, kc, r)), (tmkr, rC, rC2), (o_r,))
                        yield

            run_rr([blocks(i_) for i_ in range(nslots)] + list(extra))

        def nt_out(kc, si, s, N):
            return NT[:, kc, s:s + N], rNT(kc, si)

        def padviews(Pc, Pl, s, N, isctx, lpad):
            if isctx:
                return (Pc[:, lpad:lpad + N], lambda k: Pc[:, k:k + N], lambda ps: ps[:, 0:N])
            r0 = (s - CTX) // 64
            nr = N // 64
            return (Pl[:, r0:r0 + nr, lpad:lpad + 64], lambda k: Pl[:, r0:r0 + nr, k:k + 64],
                    lambda ps: ps[:, 0:N].rearrange("p (r c) -> p r c", c=64))

        for bi in range(BPC):
            hst = ExitStack()
            H = sb("H", [128, KC, T], F32, hst)

            def load_h_from_inputs():
                for kc in range(KC):
                    kb.dma("sp", H[:, kc, 0:CTX], ctxT[bi, kc * 128:(kc + 1) * 128, :], (), (rH(kc, 0),))
                    for q in range(4):
                        kb.dma("sp", H[:, kc, CTX + q * 512:CTX + (q + 1) * 512],
                               xT[bi, kc * 128:(kc + 1) * 128, q * 512:(q + 1) * 512], (), (rH(kc, 1 + q),))

            load_h_from_inputs()
            for l in range(DEPTH):
                last = l == DEPTH - 1
                inw = in_w[l].rearrange("(kc p) n -> p kc n", p=128)
                segs_loc = [sg for sg in SEG512 if not (last and sg[2])]
                seg_idx = {sg: i for i, sg in enumerate(SEG512)}

                with ExitStack() as phn:
                    tmp = (Rot(kb, phn, "sq", [128, 512], BF16, 3), Rot(kb, phn, "rs", [128, 512], F32, 3),
                           Rot(kb, phn, "tm", [128, 512], F32, 3))
                    extra = []
                    if bi == 0 and l == 0:
                        WA1 = Rot(kb, phn, "adaW1", [128, KC, 512], BF16, 2)
                        extra = [ada_gen(1, WA1, rC2, bankL)]
                    rms_mod(l, 0, 0, bi, SEG512, tmp, nt_out, extra=extra, nslots=3)
                if l > 0:
                    for kc in range(KC):
                        kb.dma("sp", hsp[kc], H[:, kc, :], tuple(rH(kc, i) for i in range(5)), (kb.reg("hsp", kc),))
                kb.phase_end(KEEP)
                hst.close()

                with ExitStack() as ph:
                    UFT = sb("UFT", [128, 18, 512], BF16, ph)
                    WF = sb("WF", [128, KC, 512], BF16, ph)
                    BFR = sb("BFR", [1, 512], BF16, ph)
                    DCt = sb("DCt", [128, 2, 2, 256], BF16, ph)
                    DL = Rot(kb, ph, "DL", [128, 2, 16 * 256], BF16, 2)
                    AB = Rot(kb, ph, "AB", [128, 512], BF16, 3)
                    rW = kb.reg("WF")
                    kb.dma("pool", WF[:], inw[:, :, 0:512], (), (rW,))
                    kb.dma("pool", BFR[:], inbf_d[l], (), (rW,))
                    kb.dma("pool", DCt[:, 0, :, :].rearrange("p a b -> p (a b)"), dftC[0], (), (rW,))
                    kb.dma("pool", DCt[:, 1, :, :].rearrange("p a b -> p (a b)"), dftC[1], (), (rW,))
                    DMt = sb("DMt", [128, 2, 16, 2], BF16, ph)
                    kb.dma("pool", DMt[:, 0, :, :].rearrange("p a b -> p (a b)"), dftM[0], (), (rW,))
                    kb.dma("pool", DMt[:, 1, :, :].rearrange("p a b -> p (a b)"), dftM[1], (), (rW,))
                    tcs = list(range(18)) if not last else list(range(2, 18))
                    for tc in tcs:
                        si = 0 if tc < 2 else 1 + (tc - 2) // 4
                        ps, pr = bank()
                        for kc in range(KC):
                            mm(ps[:], NT[:, kc, tc * 128:(tc + 1) * 128], WF[:, kc, :], kc == 0, False,
                               (rNT(kc, si), rW), (pr,))
                        mm(ps[:], ONES[0:1, :], BFR[0:1, :], False, True, (rC, rW), (pr,))
                        copy_any(UFT[:, tc, :], ps[:], (pr,), (kb.reg("UFT", tc),))
                    uft_regs = [kb.reg("UFT", tc) for tc in tcs]

                    def chan_dft(ab, abr, M, out_ap, out_r, mirror=None):
                        ps2, pr2 = bank()
                        mm(ps2[:, 0:M], CKS[:, 0, :], ab[:, 0:M], True, False, (abr, rC), (pr2,))
                        mm(ps2[:, 0:M], CKS[:, 1, :], ab[:, 256:256 + M], False, True, (abr, rC), (pr2,))
                        copy_any(out_ap, ps2[:, 0:M], (pr2,), (out_r,))
                        if mirror is not None:
                            m0, m_ap, m_r = mirror
                            ps3, pr3 = bank()
                            mm(ps3[:, 0:M], CKS[:, 0, :], ab[:, 0:M], True, False, (abr, rC), (pr3,))
                            mm(ps3[:, 0:M], CKS[:, 2, :], ab[:, 256:256 + M], False, True, (abr, rC), (pr3,))
                            kb.op("dve", lambda e: e.tensor_copy(out=m_ap, in_=ps3[:, m0:M]), (pr3,), (m_r,))

                    if not last:
                        for g in range(4):
                            ps, pr = bank()
                            for half in range(2):
                                for lc in range(2):
                                    mm(ps[:, half * 256:(half + 1) * 256], UFT[:, lc, g * 128:(g + 1) * 128],
                                       DCt[:, half, lc, :], lc == 0, lc == 1, (uft_regs[lc], rW), (pr,))
                            ab, abr = AB.get()
                            copy_any(ab[:], ps[:], (pr,), (abr,))
                            chan_dft(ab, abr, 256, YF[:, g, 0:256], kb.reg("YF", g, 0))
                    def load_dl(mb):
                        dl_, dlr_ = DL.get()
                        kb.dma("sp", dl_[:, 0, :], dftLb[0, mb], (rdl,), (dlr_,))
                        kb.dma("sp", dl_[:, 1, :], dftLb[1, mb], (rdl,), (dlr_,))
                        return dl_, dlr_

                    nxt_dl = load_dl(0)
                    pend_cd = None
                    for mb in range(4):
                        dl, dlr = nxt_dl
                        if mb < 3:
                            nxt_dl = load_dl(mb + 1)
                        for g in range(4):
                            ps, pr = bank()
                            for half in range(2):
                                for lc in range(16):
                                    mm(ps[:, half * 256:(half + 1) * 256],
                                       UFT[:, 2 + lc, g * 128:(g + 1) * 128],
                                       dl[:, half, lc * 256:(lc + 1) * 256], lc == 0, lc == 15,
                                       (kb.reg("UFT", 2 + lc), dlr), (pr,))
                            ab, abr = AB.get()
                            copy_any(ab[:], ps[:], (pr,), (abr,))
                            if pend_cd is not None:
                                chan_dft(*pend_cd)
                            m0 = 1 if mb == 0 else 0
                            pend_cd = (ab, abr, 256, YF[:, g, CTX + mb * 256:CTX + (mb + 1) * 256],
                                       kb.reg("YF", g, 1 + mb),
                                       (m0, YF[:, g, CTX + 1793 - mb * 256:CTX + 2049 - mb * 256 - m0][:, ::-1],
                                        kb.reg("YF", g, 5 + mb)))
                    chan_dft(*pend_cd)
                    for g in range(4):
                        ps, pr = bank()
                        for half in range(2):
                            for lc in range(16):
                                mm(ps[:, half * 256:half * 256 + 2], UFT[:, 2 + lc, g * 128:(g + 1) * 128],
                                   DMt[:, half, lc, :], lc == 0, lc == 15, (kb.reg("UFT", 2 + lc), rW), (pr,))
                        ab, abr = AB.get()
                        kb.op("dve", lambda e, ab=ab, ps=ps: e.tensor_copy(
                            out=ab[:].rearrange("p (h c) -> p h c", h=2)[:, :, 0:2],
                            in_=ps[:].rearrange("p (h c) -> p h c", h=2)[:, :, 0:2]), (pr,), (abr,))
                        ps2, pr2 = bank()
                        mm(ps2[:, 0:2], CKS[:, 0, :], ab[:, 0:2], True, False, (abr, rC), (pr2,))
                        mm(ps2[:, 0:2], CKS[:, 1, :], ab[:, 256:258], False, True, (abr, rC), (pr2,))
                        kb.op("dve", lambda e, ps2=ps2, g=g: e.tensor_copy(
                            out=YF[:, g, CTX + 1024:CTX + 1025], in_=ps2[:, 0:1]), (pr2,), (kb.reg("YF", g, 9),))
                    kb.phase_end(KEEP)

                with ExitStack() as ph:
                    WL = Rot(kb, ph, "WL", [128, 2, KC, 128], BF16, 2)
                    UXc = sb("UXc", [128, CTX + 3], BF16, ph)
                    UXl = sb("UXl", [128, 32, 67], BF16, ph)
                    XC = sb("XC", [128, T], F32, ph)
                    XCB = sb("XCB", [128, T], BF16, ph)
                    HF = sb("HF", [128, T], F32, ph)
                    HB = sb("HB", [128, T], F32, ph)
                    WG = sb("WG", [128, 2, 2, 128], BF16, ph)
                    DGL = sb("DGL", [128, 4, 128], BF16, ph)
                    SA = sb("SA", [128, 3, 8], F32, ph)
                    NB = sb("NB", [128, 16], F32, ph)
                    TP = Rot(kb, ph, "lt", [128, 512], F32, 12)
                    TC = Rot(kb, ph, "lc", [128, 256], F32, 2)
                    rux = kb.reg("UX")
                    rwg = kb.reg("WG")
                    rsa = kb.reg("SA")
                    kb.op("dve", lambda e: e.memset(UXc[:], 0.0), (), (rux,))
                    kb.op("dve", lambda e: e.memset(UXl[:], 0.0), (), (rux,))
                    kb.op("dve", lambda e: e.memset(WG[:], 0.0), (), (rwg,))
                    o = VEC["lam"]
                    kb.op("act", lambda e: e.activation(out=SA[:, 0, :], in_=VECS[:, l, o:o + 8], func=AF.Exp,
                                                        scale=-1.0), (rC,), (rsa,))
                    kb.op("act", lambda e: e.activation(out=SA[:, 0, :], in_=SA[:, 0, :], func=AF.Ln,
                                                        bias=ONE1), (rsa, rC), (rsa,))
                    kb.op("dve", lambda e: e.tensor_scalar(out=SA[:, 1, :], in0=SA[:, 0, :], scalar1=-8.0,
                                                           scalar2=None, op0=ALU.mult), (rsa,), (rsa,))
                    kb.op("dve", lambda e: e.tensor_scalar(out=SA[:, 2, :], in0=SA[:, 0, :], scalar1=-16.0,
                                                           scalar2=None, op0=ALU.mult), (rsa,), (rsa,))
                    o2 = VEC["lba"]
                    kb.op("dve", lambda e: e.tensor_scalar(out=NB[:, :], in0=VECS[:, l, o2:o2 + 16], scalar1=-1.0,
                                                           scalar2=None, op0=ALU.mult), (rC,), (rsa,))
                    WC = Rot(kb, ph, "WC", [128, 2, KC, 128], BF16, 2)
                    DGC = sb("DGC", [128, 31, 128], BF16, ph)
                    VPc = sb("VPc", [128, CTX + 30], BF16, ph)
                    VPl = sb("VPl", [128, 32, 94], BF16, ph)
                    TS = Rot(kb, ph, "ct", [128, 512], F32, 2)
                    TL = Rot(kb, ph, "cl", [128, 256], F32, 6)
                    SQ = Rot(kb, ph, "csq", [128, 256], BF16, 2)
                    rvp = kb.reg("VP")
                    NBC = sb("NBC", [128, 4], F32, ph)
                    o3 = VEC["in_b"] + 16
                    rnbc = kb.reg("NBC")
                    kb.op("dve", lambda e: e.tensor_scalar(out=NBC[:, :], in0=VECS[:, l, o3:o3 + 4], scalar1=-1.0,
                                                           scalar2=None, op0=ALU.mult), (rC,), (rnbc,))
                    kb.op("dve", lambda e: e.memset(VPc[:], 0.0), (), (rvp,))
                    kb.op("dve", lambda e: e.memset(VPl[:], 0.0), (), (rvp,))

                    def lru_gen():
                        def load_wl(c_):
                            wl_, wlr_ = WL.get()
                            kb.dma("pool", wl_[:, 0, :, :], inw[:, :, OFF_LX + c_ * 128:OFF_LX + (c_ + 1) * 128], (), (wlr_,))
                            kb.dma("pool", wl_[:, 1, :, :], inw[:, :, OFF_LG + c_ * 128:OFF_LG + (c_ + 1) * 128], (), (wlr_,))
                            return wl_, wlr_

                        nxt_wl = load_wl(0)
                        for c in range(4):
                            wl, wlr = nxt_wl
                            if c < 3:
                                nxt_wl = load_wl(c + 1)
                            for d in range(2):
                                for gi, wsrc in enumerate((lwa, lwx)):
                                    for hh in range(2):
                                        kb.dma("pool", WG[hh * 64:(hh + 1) * 64, d, gi, hh * 64:(hh + 1) * 64],
                                               wsrc[l, d, 2 * c + hh], (), (rwg,))
                            for k in range(4):
                                kb.op("dve", lambda e, k=k, c=c: e.tensor_scalar(
                                    out=DGL[:, k, :], in0=IDB[:], scalar1=vec(l, "lcw", c * 4 + k), scalar2=None,
                                    op0=ALU.mult), (rC,), (kb.reg("DGL"),))
                            for si, (s, N, isctx) in enumerate(SEG512):
                                interior, tap, psv = padviews(UXc, UXl, s, N, isctx, 2)
                                ps, pr = bankL()
                                for kc in range(KC):
                                    mm(ps[:, 0:N], wl[:, 0, kc, :], NT[:, kc, s:s + N], kc == 0, kc == KC - 1,
                                       (wlr, rNT(kc, si)), (pr,))
                                kb.op("dve", lambda e, interior=interior, psv=psv, ps=ps, c=c: e.tensor_scalar(
                                    out=interior, in0=psv(ps), scalar1=vec(l, "in_b", 4 + c), scalar2=None, op0=ALU.add),
                                    (pr, rC), (kb.reg("UXs", si),))
                                yield
                            gl = []
                            for si, (s, N, isctx) in enumerate(SEG512):
                                if last and isctx:
                                    continue
                                ps2, pr2 = bankL()
                                for kc in range(KC):
                                    mm(ps2[:, 0:N], wl[:, 1, kc, :], NT[:, kc, s:s + N], kc == 0, kc == KC - 1,
                                       (wlr, rNT(kc, si)), (pr2,))
                                gl.append((ps2, pr2, si, s, N))
                                if len(gl) == 4 or si == 4:
                                    for (ps2, pr2, si_, s_, N_) in gl:
                                        kb.op("act", lambda e, ps2=ps2, c=c, s_=s_, N_=N_: e.activation(
                                            out=YR[:, c, s_:s_ + N_], in_=ps2[:, 0:N_], func=AF.Gelu, bias=vec(l, "in_b", 8 + c)),
                                            (pr2, rC), (kb.reg("YR", c, si_),))
                                    gl = []
                            yield
                            for si, (s, N, isctx) in enumerate(SEG512):
                                interior, tap, psv = padviews(UXc, UXl, s, N, isctx, 2)
                                ps, pr = bankL()
                                for k in range(4):
                                    mm(psv(ps), DGL[:, k, :], tap(k), k == 0, k == 3,
                                       (kb.reg("DGL"), kb.reg("UXs", si), rux), (pr,))
                                kb.op("dve", lambda e, ps=ps, s=s, N=N, c=c: e.tensor_scalar(
                                    out=XC[:, s:s + N], in0=ps[:, 0:N], scalar1=vec(l, "lcb", c), scalar2=None, op0=ALU.add),
                                    (pr, rC), (kb.reg("XC", si),))
                                kb.op("dve", lambda e, ps=ps, s=s, N=N, c=c: e.tensor_scalar(
                                    out=XCB[:, s:s + N], in0=ps[:, 0:N], scalar1=vec(l, "lcb", c), scalar2=None, op0=ALU.add),
                                    (pr, rC), (kb.reg("XCB", si),))
                                yield
                            orders = {0: [0, 1, 2, 3, 4], 1: [0, 4, 3, 2, 1]}
                            ready = {0: {}, 1: {}}
                            scanned = {0: 0, 1: 0}

                            def gates(d, par, c=c):
                                q = 2 * d + par
                                tr, trr = TP.t[3 * q], kb.reg("lt", 3 * q)
                                ti, tir = TP.t[3 * q + 1], kb.reg("lt", 3 * q + 1)
                                ta, tar = TP.t[3 * q + 2], kb.reg("lt", 3 * q + 2)
                                psb, prb = PS[q], kb.reg("ps", q)
                                for oi in range(par, 5, 2):
                                    si = orders[d][oi]
                                    while scanned[d] < oi - 1:
                                        yield
                                    s, N, isctx = SEG512[si]
                                    mm(psb[:, 0:N], WG[:, d, 0, :], XCB[:, s:s + N], True, True, (rwg, kb.reg("XCB", si)), (prb,))
                                    yield
                                    kb.op("act", lambda e: e.activation(
                                        out=tr[:, 0:N], in_=psb[:, 0:N], func=AF.Exp, scale=-1.0, bias=NB[:, d * 4 + c:d * 4 + c + 1]),
                                        (prb, rsa), (trr,))
                                    mm(psb[:, 0:N], WG[:, d, 1, :], XCB[:, s:s + N], True, True, (rwg, kb.reg("XCB", si)), (prb,))
                                    yield
                                    kb.op("act", lambda e: e.activation(
                                        out=ti[:, 0:N], in_=psb[:, 0:N], func=AF.Exp, scale=-1.0, bias=NB[:, 8 + d * 4 + c:8 + d * 4 + c + 1]),
                                        (prb, rsa), (tir,))
                                    yield
                                    kb.op("act", lambda e: e.activation(
                                        out=tr[:, 0:N], in_=tr[:, 0:N], func=AF.Ln, bias=ONE1), (trr, rC), (trr,))
                                    yield
                                    kb.op("act", lambda e: e.activation(
                                        out=ti[:, 0:N], in_=ti[:, 0:N], func=AF.Ln, bias=ONE1), (tir, rC), (tir,))
                                    yield
                                    kb.op("act", lambda e: e.activation(
                                        out=tr[:, 0:N], in_=tr[:, 0:N], func=AF.Exp, scale=-1.0), (trr,), (trr,))
                                    yield
                                    kb.op("act", lambda e: e.activation(
                                        out=ti[:, 0:N], in_=ti[:, 0:N], func=AF.Exp, scale=-1.0), (tir,), (tir,))
                                    yield
                                    kb.op("act", lambda e: e.activation(
                                        out=ta[:, 0:N], in_=tr[:, 0:N], func=AF.Exp, scale=SA[:, 1, d * 4 + c:d * 4 + c + 1]),
                                        (trr, rsa), (tar,))
                                    kb.op("dve", lambda e: e.tensor_tensor(
                                        out=ti[:, 0:N], in0=ti[:, 0:N], in1=XC[:, s:s + N], op=ALU.mult),
                                        (tir, kb.reg("XC", si)), (tir,))
                                    yield
                                    kb.op("dve", lambda e: e.tensor_tensor(
                                        out=tr[:, 0:N], in0=ta[:, 0:N], in1=ta[:, 0:N], op=ALU.mult), (tar,), (trr,))
                                    yield
                                    kb.op("act", lambda e: e.activation(
                                        out=tr[:, 0:N], in_=tr[:, 0:N], func=AF.Ln, scale=-1.0, bias=ONE1),
                                        (trr, rC), (trr,))
                                    yield
                                    kb.op("act", lambda e: e.activation(
                                        out=tr[:, 0:N], in_=tr[:, 0:N], func=AF.Exp, scale=0.5), (trr,), (trr,))
                                    yield
                                    kb.op("dve", lambda e: e.tensor_tensor(
                                        out=ti[:, 0:N], in0=ti[:, 0:N], in1=tr[:, 0:N], op=ALU.mult),
                                        (tir, trr), (tir,))
                                    ready[d][oi] = (ta, tar, ti, tir)
                                    yield

                            def scans(d):
                                HD = HF if d == 0 else HB
                                hname = "HF" if d == 0 else "HB"
                                prev = None
                                for oi, si in enumerate(orders[d]):
                                    while oi not in ready[d]:
                                        yield
                                    ta, tar, ti, tir = ready[d][oi]
                                    s, N, isctx = SEG512[si]
                                    init = 0.0 if prev is None else prev[0]
                                    rd = (tar, tir) + (() if prev is None else (prev[1],))
                                    if d == 0:
                                        kb.op("dve", lambda e: e.tensor_tensor_scan(
                                            out=HD[:, s:s + N], data0=ta[:, 0:N], data1=ti[:, 0:N], initial=init,
                                            op0=ALU.mult, op1=ALU.add), rd, (kb.reg(hname, si),))
                                        prev = (HD[:, s + N - 1:s + N], kb.reg(hname, si))
                                    else:
                                        kb.op("dve", lambda e: e.tensor_tensor_scan(
                                            out=HD[:, s:s + N][:, ::-1], data0=ta[:, 0:N][:, ::-1], data1=ti[:, 0:N][:, ::-1],
                                            initial=init, op0=ALU.mult, op1=ALU.add), rd, (kb.reg(hname, si),))
                                        prev = (HD[:, s:s + 1], kb.reg(hname, si))
                                    scanned[d] = oi + 1
                                    yield

                            yield from rr_gen([gates(0, 0), gates(1, 0), gates(0, 1), gates(1, 1), scans(0), scans(1)])
                            for si, (s, N, isctx) in enumerate(SEG512):
                                if last and isctx:
                                    continue
                                for h0 in range(0, N, 256):
                                    tcb, tcr = TC.get()
                                    kb.op("pool", lambda e, tcb=tcb, s=s, h0=h0: e.tensor_tensor(
                                        out=tcb[:, 0:256], in0=HB[:, s + h0:s + h0 + 256], in1=HF[:, s + h0:s + h0 + 256], op=ALU.add),
                                        (kb.reg("HB", si), kb.reg("HF", si)), (tcr,))
                                    kb.op("pool", lambda e, tcb=tcb, s=s, h0=h0, c=c: e.tensor_tensor(
                                        out=YR[:, c, s + h0:s + h0 + 256], in0=tcb[:, 0:256], in1=YR[:, c, s + h0:s + h0 + 256], op=ALU.mult),
                                        (tcr, kb.reg("YR", c, si)), (kb.reg("YR", c, si),))
                                    yield

                    def conf_gen():
                        def load_wc(c_):
                            wc_, wcr_ = WC.get()
                            kb.dma("pool", wc_[:, 0, :, :], inw[:, :, OFF_C + c_ * 128:OFF_C + (c_ + 1) * 128], (), (wcr_,))
                            kb.dma("pool", wc_[:, 1, :, :], inw[:, :, OFF_C + 512 + c_ * 128:OFF_C + 512 + (c_ + 1) * 128], (), (wcr_,))
                            return wc_, wcr_

                        nxt_wc = load_wc(0)
                        for c in range(4):
                            wc, wcr = nxt_wc
                            if c < 3:
                                nxt_wc = load_wc(c + 1)
                            for k in range(31):
                                kb.op("dve", lambda e, k=k, c=c: e.tensor_scalar(
                                    out=DGC[:, k, :], in0=IDB[:], scalar1=vec(l, "ccw", c * 31 + k), scalar2=None,
                                    op0=ALU.mult), (rC,), (kb.reg("DGC", k),))
                                if k % 8 == 7:
                                    yield
                            for (s, N, isctx) in segs_loc:
                                si = seg_idx[(s, N, isctx)]
                                interior, tap, psv = padviews(VPc, VPl, s, N, isctx, 15)
                                ps_a, pr_a = bankC()
                                for kc in range(KC):
                                    mm(ps_a[:, 0:N], wc[:, 0, kc, :], NT[:, kc, s:s + N], kc == 0, kc == KC - 1,
                                       (wcr, rNT(kc, si)), (pr_a,))
                                yield
                                ps_g, pr_g = bankC()
                                for kc in range(KC):
                                    mm(ps_g[:, 0:N], wc[:, 1, kc, :], NT[:, kc, s:s + N], kc == 0, kc == KC - 1,
                                       (wcr, rNT(kc, si)), (pr_g,))
                                sg, sgr = TS.get()
                                kb.op("act", lambda e, sg=sg, ps_g=ps_g, N=N, c=c: e.activation(
                                    out=sg[:, 0:N], in_=ps_g[:, 0:N], func=AF.Exp, scale=-1.0, bias=NBC[:, c:c + 1]),
                                    (pr_g, rnbc), (sgr,))
                                yield
                                kb.op("act", lambda e, sg=sg, N=N: e.activation(
                                    out=sg[:, 0:N], in_=sg[:, 0:N], func=AF.Ln, bias=ONE1), (sgr, rC), (sgr,))
                                yield
                                kb.op("act", lambda e, sg=sg, N=N: e.activation(
                                    out=sg[:, 0:N], in_=sg[:, 0:N], func=AF.Exp, scale=-1.0), (sgr,), (sgr,))
                                yield
                                sgv = sg[:, 0:N] if isctx else sg[:, 0:N].rearrange("p (r c) -> p r c", c=64)
                                kb.op("dve", lambda e, interior=interior, psv=psv, ps_a=ps_a, sgv=sgv, c=c: e.scalar_tensor_tensor(
                                    out=interior, in0=psv(ps_a), scalar=vec(l, "in_b", 12 + c), in1=sgv,
                                    op0=ALU.add, op1=ALU.mult), (pr_a, sgr, rC), (kb.reg("VPs", si),))
                            for (s, N, isctx) in segs_loc:
                                si = seg_idx[(s, N, isctx)]
                                interior, tap, psv = padviews(VPc, VPl, s, N, isctx, 15)
                                ps, pr = bankC()
                                for k in range(31):
                                    mm(psv(ps), DGC[:, k, :], tap(k), k == 0, k == 30,
                                       (kb.reg("DGC", k), kb.reg("VPs", si), rvp), (pr,))
                                    if k % 4 == 3:
                                        yield
                                kb.op("dve", lambda e, ps=ps, s=s, N=N, c=c: e.tensor_scalar(
                                    out=YC[:, c, s:s + N], in0=ps[:, 0:N], scalar1=vec(l, "ccb", c), scalar2=None, op0=ALU.add),
                                    (pr, rC), (kb.reg("YC", c, si),))
                        segs_ln = [sg for sg in SEG256 if not (last and sg[2])]

                        def ln_gen(slot):
                            mean, mr = TL.t[3 * slot], kb.reg("cl", 3 * slot)
                            var, vr = TL.t[3 * slot + 1], kb.reg("cl", 3 * slot + 1)
                            t1, t1r = TL.t[3 * slot + 2], kb.reg("cl", 3 * slot + 2)
                            sq, sqr = SQ.t[slot], kb.reg("csq", slot)
                            ps1, pr1 = PS[4 + 2 * slot], kb.reg("ps", 4 + 2 * slot)
                            ps2, pr2 = PS[5 + 2 * slot], kb.reg("ps", 5 + 2 * slot)
                            for (s, N, isctx) in segs_ln[slot::2]:
                                s5 = 0 if isctx else 1 + (s - CTX) // 512
                                for c in range(4):
                                    mm(ps1[:, 0:N], ONES[:], YC[:, c, s:s + N], c == 0, c == 3, (kb.reg("YC", c, s5), rC), (pr1,))
                                yield
                                for c in range(4):
                                    kb.op("dve", lambda e: e.tensor_tensor(
                                        out=sq[:, 0:N], in0=YC[:, c, s:s + N], in1=YC[:, c, s:s + N], op=ALU.mult),
                                        (kb.reg("YC", c, s5),), (sqr,))
                                    mm(ps2[:, 0:N], ONES[:], sq[:, 0:N], c == 0, c == 3, (sqr, rC), (pr2,))
                                    yield
                                kb.op("act", lambda e: e.activation(
                                    out=mean[:, 0:N], in_=ps1[:, 0:N], func=AF.Identity, scale=1.0 / 512), (pr1,), (mr,))
                                yield
                                kb.op("dve", lambda e: e.tensor_tensor(
                                    out=var[:, 0:N], in0=mean[:, 0:N], in1=mean[:, 0:N], op=ALU.mult), (mr,), (vr,))
                                yield
                                kb.op("dve", lambda e: e.scalar_tensor_tensor(
                                    out=var[:, 0:N], in0=ps2[:, 0:N], scalar=1.0 / 512, in1=var[:, 0:N],
                                    op0=ALU.mult, op1=ALU.subtract), (pr2, vr), (vr,))
                                yield
                                kb.op("act", lambda e: e.activation(
                                    out=var[:, 0:N], in_=var[:, 0:N], func=AF.Ln, bias=EPS_LN), (vr, rC), (vr,))
                                yield
                                kb.op("act", lambda e: e.activation(
                                    out=var[:, 0:N], in_=var[:, 0:N], func=AF.Exp, scale=-0.5), (vr,), (vr,))
                                yield
                                for c in range(4):
                                    kb.op("dve", lambda e: e.tensor_tensor(
                                        out=t1[:, 0:N], in0=YC[:, c, s:s + N], in1=mean[:, 0:N], op=ALU.subtract),
                                        (kb.reg("YC", c, s5), mr), (t1r,))
                                    yield
                                    kb.op("dve", lambda e: e.tensor_tensor(
                                        out=t1[:, 0:N], in0=t1[:, 0:N], in1=var[:, 0:N], op=ALU.mult), (t1r, vr), (t1r,))
                                    yield
                                    kb.op("act", lambda e: e.activation(
                                        out=YC[:, c, s:s + N], in_=t1[:, 0:N], func=AF.Silu, scale=vec(l, "clg", c),
                                        bias=vec(l, "clb", c)), (t1r, rC), (kb.reg("YC", c, s5),))
                                    yield

                        yield from rr_gen([ln_gen(0), ln_gen(1)])

                    run_rr([conf_gen(), lru_gen()])
                    kb.phase_end(KEEP)

                hst = ExitStack()
                H = sb("H", [128, KC, T], F32, hst)
                if l == 0:
                    load_h_from_inputs()
                else:
                    for kc in range(KC):
                        kb.dma("sp", H[:, kc, :], hsp[kc], (kb.reg("hsp", kc),), tuple(rH(kc, i) for i in range(5)))

                with ExitStack() as ph:
                    WGJ = Rot(kb, ph, "WGJ", [128, 3, KC, 128], BF16, 2)
                    WOJ = Rot(kb, ph, "WOJ", [128, 3, 4, 128], BF16, 2)
                    MG = Rot(kb, ph, "MG", [128, T], BF16, 2)
                    WMX = Rot(kb, ph, "WMX", [128, 2, D], BF16, 1)
                    TG = Rot(kb, ph, "mt", [128, 512], F32, 3)
                    mixw = mix_w[l].rearrange("(j p) n -> p j n", p=128)
                    outs = [w_[l].rearrange("(kc p) n -> p kc n", p=128) for w_ in (fo_w, lo_w, co_w)]
                    YS = (YF, YR, YC)
                    YN = ("YF", "YR", "YC")
                    pair = []

                    def load_j(j):
                        wg, wgr = WGJ.get()
                        wo, wor = WOJ.get()
                        for k3 in range(3):
                            c0 = OFF_G + k3 * D + j * 128
                            kb.dma("pool", wg[:, k3, :, :], inw[:, :, c0:c0 + 128], (), (wgr,))
                            kb.dma("pool", wo[:, k3, :, :], outs[k3][:, :, j * 128:(j + 1) * 128], (), (wor,))
                        return wg, wgr, wo, wor

                    nxt = load_j(0)
                    for j in range(8):
                        wg, wgr, wo, wor = nxt
                        if j % 2 == 0:
                            wm, wmr = WMX.get()
                            kb.dma("pool", wm[:], mixw[:, j:j + 2, :], (), (wmr,))
                        if j < 7:
                            nxt = load_j(j + 1)
                        mg, mgr_base = MG.get()
                        pair.append((mg, mgr_base))
                        for (s, N, isctx) in segs_loc:
                            si = seg_idx[(s, N, isctx)]
                            tg = []
                            for k3 in range(3):
                                ps, pr = bank()
                                for kc in range(KC):
                                    mm(ps[:, 0:N], wg[:, k3, kc, :], NT[:, kc, s:s + N], kc == 0, kc == KC - 1,
                                       (wgr, rNT(kc, si)), (pr,))
                                t, tr_ = TG.get()
                                kb.op("act", lambda e, t=t, ps=ps, N=N, k3=k3, j=j: e.activation(
                                    out=t[:, 0:N], in_=ps[:, 0:N], func=AF.Sigmoid,
                                    bias=vec(l, "in_b", 20 + k3 * 8 + j)), (pr, rC), (tr_,))
                                ps2, pr2 = bank()
                                for c in range(4):
                                    yreg = kb.reg(YN[k3], c, si) if k3 != 0 else None
                                    rds = [wor]
                                    if k3 == 0:
                                        rds += [kb.reg("YF", c, i) for i in (range(0, 1) if isctx else range(1, 9))]
                                    elif k3 == 1:
                                        rds += [kb.reg("YR", c, i) for i in (range(0, 1) if isctx else range(1, 9))]
                                    else:
                                        rds.append(yreg)
                                    mm(ps2[:, 0:N], wo[:, k3, c, :], YS[k3][:, c, s:s + N], c == 0, c == 3, rds, (pr2,))
                                kb.op("dve", lambda e, t=t, ps2=ps2, N=N: e.tensor_tensor(
                                    out=t[:, 0:N], in0=t[:, 0:N], in1=ps2[:, 0:N], op=ALU.mult), (tr_, pr2), (tr_,))
                                tg.append((t, tr_))
                            kb.op("dve", lambda e, tg=tg, N=N: e.tensor_tensor(
                                out=tg[0][0][:, 0:N], in0=tg[0][0][:, 0:N], in1=tg[1][0][:, 0:N], op=ALU.add),
                                (tg[0][1], tg[1][1]), (tg[0][1],))
                            kb.op("dve", lambda e, tg=tg, mg=mg, s=s, N=N: e.tensor_tensor(
                                out=mg[:, s:s + N], in0=tg[0][0][:, 0:N], in1=tg[2][0][:, 0:N], op=ALU.add),
                                (tg[0][1], tg[2][1]), (kb.reg("MGs", j % 3, si), mgr_base))
                        if j % 2 == 1:
                            q = j // 2
                            for m in range(KC):
                                for (s, N, isctx) in segs_loc:
                                    si = seg_idx[(s, N, isctx)]
                                    r = 2 if isctx else bi
                                    ps, pr = bank()
                                    for jj in range(2):
                                        mm(ps[:, 0:N], wm[:, jj, m * 128:(m + 1) * 128], pair[jj][0][:, s:s + N],
                                           jj == 0, jj == 1, (wmr, pair[jj][1]), (pr,))
                                    kb.op("dve", lambda e, ps=ps, m=m, s=s, N=N, r=r: e.scalar_tensor_tensor(
                                        out=H[:, m, s:s + N], in0=ps[:, 0:N], scalar=mod(l, 2, m, r),
                                        in1=H[:, m, s:s + N], op0=ALU.mult, op1=ALU.add),
                                        (pr, rC, rC2, rH(m, si)), (rH(m, si),))
                            pair = []
                    kb.phase_end(KEEP)

                with ExitStack() as ph:
                    WU = ViewRot(kb, "WU", [BREG[:, 8 * T + i * 4096:8 * T + (i + 1) * 4096].rearrange(
                        "p (a k n) -> p a k n", a=2, k=KC) for i in range(2)])
                    WD = Rot(kb, ph, "WD", [128, 4, D], BF16, 1)
                    DGF = Rot(kb, ph, "DGF", [128, 2, 3, 128], BF16, 2)
                    UPc = sb("UPc", [128, 2, CTX + 2], BF16, ph)
                    UPl = sb("UPl", [128, 2, 32, 66], BF16, ph)
                    AC = ViewRot(kb, "AC", [YF, YR])
                    TF = Rot(kb, ph, "ft", [128, 512], F32, 2)
                    rup = kb.reg("UP")
                    kb.op("dve", lambda e: e.memset(UPc[:], 0.0), (), (rup,))
                    kb.op("dve", lambda e: e.memset(UPl[:], 0.0), (), (rup,))
                    with ExitStack() as phn:
                        tmp = (Rot(kb, phn, "sq", [128, 512], BF16, 3), Rot(kb, phn, "rs", [128, 512], F32, 3),
                               Rot(kb, phn, "tm", [128, 512], F32, 3))
                        rms_mod(l, 1, 3, bi, segs_loc, tmp, nt_out, nslots=3)
                    upw = up_w[l].rearrange("(kc p) n -> p kc n", p=128)
                    dnw = dn_w[l].rearrange("(j p) n -> p j n", p=128)
                    NJ = FH // 128
                    ac = acr = None
                    wu = wur = None
                    def load_wu(j):
                        wu_, wur_ = WU.get()
                        kb.dma("pool", wu_[:, 0, :, :], upw[:, :, j * 128:(j + 2) * 128], (), (wur_,))
                        kb.dma("pool", wu_[:, 1, :, :], upw[:, :, FH + j * 128:FH + (j + 2) * 128], (), (wur_,))
                        return wu_, wur_

                    nxt_wu = load_wu(0)
                    for j in range(NJ):
                        if j % 2 == 0:
                            wu, wur = nxt_wu
                            if j + 2 < NJ:
                                nxt_wu = load_wu(j + 2)
                        if j % 4 == 0:
                            ac, acr = AC.get()
                            g0 = j
                            ng = min(4, NJ - j)
                            wd, wdr = WD.get()
                            kb.dma("pool", wd[:, 0:ng, :], dnw[:, g0:g0 + ng, :], (), (wdr,))
                        jj = j % 2
                        dg, dgr = DGF.get()
                        for vg in range(2):
                            for k in range(3):
                                cc = vg * NJ + j
                                kb.op("dve", lambda e, dg=dg, vg=vg, k=k, cc=cc: e.tensor_scalar(
                                    out=dg[:, vg, k, :], in0=IDB[:], scalar1=vec(l, "fcw", cc * 3 + k), scalar2=None,
                                    op0=ALU.mult), (rC,), (dgr,))
                        def ffn_up(seg):
                            s, N, isctx = seg
                            si = seg_idx[seg]
                            for vg in range(2):
                                interior, tap, psv = padviews(UPc[:, vg, :], UPl[:, vg, :, :], s, N, isctx, 1)
                                ps, pr = bank()
                                for kc in range(KC):
                                    mm(ps[:, 0:N], wu[:, vg, kc, jj * 128:(jj + 1) * 128], NT[:, kc, s:s + N],
                                       kc == 0, kc == KC - 1, (wur, rNT(kc, si)), (pr,))
                                copy_any(interior, psv(ps), (pr,), (kb.reg("UPs", vg, si),))

                        def ffn_conv(seg):
                            s, N, isctx = seg
                            si = seg_idx[seg]
                            pcs = []
                            for vg in range(2):
                                interior, tap, psv = padviews(UPc[:, vg, :], UPl[:, vg, :, :], s, N, isctx, 1)
                                ps, pr = bank()
                                for k in range(3):
                                    mm(psv(ps), dg[:, vg, k, :], tap(k), k == 0, k == 2,
                                       (dgr, kb.reg("UPs", vg, si), rup), (pr,))
                                pcs.append((ps, pr))
                            t, tr_ = TF.get()
                            kb.op("act", lambda e: e.activation(
                                out=t[:, 0:N], in_=pcs[1][0][:, 0:N], func=AF.Silu, bias=vec(l, "fcb", NJ + j)),
                                (pcs[1][1], rC), (tr_,))
                            kb.op("dve", lambda e: e.scalar_tensor_tensor(
                                out=ac[:, j % 4, s:s + N], in0=pcs[0][0][:, 0:N], scalar=vec(l, "fcb", j),
                                in1=t[:, 0:N], op0=ALU.add, op1=ALU.mult), (pcs[0][1], tr_, rC),
                                (kb.reg("ACs", j % 4, si), acr))

                        for bi_, seg in enumerate(segs_loc):
                            ffn_up(seg)
                            if bi_ > 0:
                                ffn_conv(segs_loc[bi_ - 1])
                        ffn_conv(segs_loc[-1])
                        if j % 4 == 3 or j == NJ - 1:
                            for m in range(KC):
                                for (s, N, isctx) in segs_loc:
                                    si = seg_idx[(s, N, isctx)]
                                    r = 2 if isctx else bi
                                    ps, pr = bank()
                                    for q in range(ng):
                                        mm(ps[:, 0:N], wd[:, q, m * 128:(m + 1) * 128], ac[:, q, s:s + N],
                                           q == 0, q == ng - 1, (wdr, acr), (pr,))
                                    kb.op("dve", lambda e, ps=ps, m=m, s=s, N=N, r=r: e.scalar_tensor_tensor(
                                        out=H[:, m, s:s + N], in0=ps[:, 0:N], scalar=mod(l, 5, m, r),
                                        in1=H[:, m, s:s + N], op0=ALU.mult, op1=ALU.add),
                                        (pr, rC, rC2, rH(m, si)), (rH(m, si),))
                    kb.phase_end(KEEP)

            with ExitStack() as ph:
                tmp = (Rot(kb, ph, "sq", [128, 512], BF16, 3), Rot(kb, ph, "rs", [128, 512], F32, 2),
                       Rot(kb, ph, "tm", [128, 512], F32, 3))
                OUTB = Rot(kb, ph, "ob", [128, 512], F32, 4)
                lat = [sg for sg in SEG512 if not sg[2]]
                for sg in lat:
                    s, N, _ = sg
                    si = SEG512.index(sg)
                    ps, pr = bank()
                    SQ, RS, TM = tmp
                    for kc in range(KC):
                        sq, sqr = SQ.get()
                        kb.op("act", lambda e, sq=sq, kc=kc, s=s, N=N: e.activation(
                            out=sq[:, 0:N], in_=H[:, kc, s:s + N], func=AF.Square), (rH(kc, si),), (sqr,))
                        mm(ps[:, 0:N], ONES[:], sq[:, 0:N], kc == 0, kc == KC - 1, (sqr, rC), (pr,))
                    rs, rsr = RS.get()
                    kb.op("act", lambda e, rs=rs, ps=ps, N=N: e.activation(
                        out=rs[:, 0:N], in_=ps[:, 0:N], func=AF.Sqrt, scale=1.0 / D, bias=EPS_RMS), (pr, rC), (rsr,))
                    kb.op("dve", lambda e, rs=rs, N=N: e.reciprocal(out=rs[:, 0:N], in_=rs[:, 0:N]), (rsr,), (rsr,))
                    for kc in range(KC):
                        ob, obr = OUTB.get()
                        kb.op("dve", lambda e, ob=ob, rs=rs, kc=kc, s=s, N=N: e.scalar_tensor_tensor(
                            out=ob[:, 0:N], in0=H[:, kc, s:s + N], scalar=vec(0, "fing", kc), in1=rs[:, 0:N],
                            op0=ALU.mult, op1=ALU.mult), (rH(kc, si), rsr, rC), (obr,))
                        kb.dma("sp", outT[bi, kc * 128:(kc + 1) * 128, s - CTX:s - CTX + N], ob[:, 0:N], (obr,), ())
                kb.phase_end(KEEP)
            hst.close()
        kb.barrier()
    return nc


_CACHE = {}


def _fm(v, n):
    return np.ascontiguousarray(np.asarray(v, np.float32).reshape(n, 128).T)


def _fm_conv(w, n):
    K = w.shape[0]
    return np.ascontiguousarray(np.asarray(w, np.float32).reshape(K, n, 128).transpose(2, 1, 0).reshape(128, n * K))


def _dft_consts():
    if "dft" in _CACHE:
        return _CACHE["dft"]
    L = SEQ
    idx = (np.arange(L, dtype=np.int64)[:, None] * np.arange(L, dtype=np.int64)[None, :]) % L
    ang = 2.0 * np.pi * idx.astype(np.float64) / L
    sc = 1.0 / np.sqrt(L)
    mats = []
    mids = []
    for fn in (np.cos, np.sin):
        Mx = (fn(ang) * sc).astype(np.float32)
        mids.append(np.ascontiguousarray(Mx[:, L // 2:L // 2 + 2].reshape(16, 128, 2).transpose(1, 0, 2).reshape(128, 32)))
        Mx = Mx.reshape(16, 128, 8, 256).transpose(2, 1, 0, 3).reshape(8, 128, 16 * 256)[0:4]
        mats.append(np.ascontiguousarray(Mx))
    dftL = np.stack(mats)
    dftM = np.stack(mids)
    Lc = CTX
    idx = (np.arange(Lc)[:, None] * np.arange(Lc)[None, :]) % Lc
    ang = 2.0 * np.pi * idx / Lc
    sc = 1.0 / np.sqrt(Lc)
    mats = []
    for fn in (np.cos, np.sin):
        Mx = (fn(ang) * sc).astype(np.float32).reshape(2, 128, 256).transpose(1, 0, 2).reshape(128, 512)
        mats.append(np.ascontiguousarray(Mx))
    dftC = np.stack(mats)
    Kc = 128
    idx = (np.arange(Kc)[:, None] * np.arange(Kc)[None, :]) % Kc
    ang = 2.0 * np.pi * idx / Kc
    sc = 1.0 / np.sqrt(Kc)
    dftK = np.stack([(np.cos(ang) * sc).astype(np.float32), (-np.sin(ang) * sc).astype(np.float32),
                     (np.sin(ang) * sc).astype(np.float32)])
    _CACHE["dft"] = (dftL, dftC, dftK, dftM)
    return _CACHE["dft"]


def kernel(x, c, ctx, c_ctx, ada_w, ada_b, norm1_g, norm2_g, in_w, in_b, fourier_out_w,
           lru_conv_w, lru_conv_b, lru_wa, lru_ba, lru_wx, lru_bx, lru_lam, lru_out_w,
           conf_conv_w, conf_conv_b, conf_ln_g, conf_ln_b, conf_out_w, mix_out_w,
           ffn_up_w, ffn_conv_w, ffn_conv_b, ffn_down_w, final_g):
    f = lambda a: np.ascontiguousarray(np.asarray(a, np.float32))
    x = f(x); ctx = f(ctx); c = f(c); c_ctx = f(c_ctx)
    B = x.shape[0]
    vecs = np.zeros((128, DEPTH, NV), np.float32)
    for l in range(DEPTH):
        def put(name, arr):
            vecs[:, l, VEC[name]:VEC[name] + arr.shape[1]] = arr
        put("ada_b", _fm(ada_b[l], 48)); put("n1g", _fm(norm1_g[l], 8)); put("n2g", _fm(norm2_g[l], 8))
        put("in_b", _fm(in_b[l], 44)); put("lcw", _fm_conv(np.asarray(lru_conv_w[l]), 4)); put("lcb", _fm(lru_conv_b[l], 4))
        put("lba", _fm(np.asarray(lru_ba[l]).reshape(-1), 8)); put("lbx", _fm(np.asarray(lru_bx[l]).reshape(-1), 8))
        put("lam", _fm(np.asarray(lru_lam[l]).reshape(-1), 8))
        put("ccw", _fm_conv(np.asarray(conf_conv_w[l]), 4)); put("ccb", _fm(conf_conv_b[l], 4))
        put("clg", _fm(conf_ln_g[l], 4)); put("clb", _fm(conf_ln_b[l], 4))
        put("fcw", _fm_conv(np.asarray(ffn_conv_w[l]), 44)); put("fcb", _fm(ffn_conv_b[l], 44))
        put("fing", _fm(final_g, 8))
    inbf = f(np.asarray(in_b)[:, None, 0:512])
    dftL, dftC, dftK, dftM = _dft_consts()
    shared = {
        "vecs": vecs, "inbf": inbf, "ada_w": f(ada_w), "in_w": f(in_w), "fourier_out_w": f(fourier_out_w),
        "lru_out_w": f(lru_out_w), "conf_out_w": f(conf_out_w), "mix_out_w": f(mix_out_w),
        "ffn_up_w": f(ffn_up_w), "ffn_down_w": f(ffn_down_w), "lru_wa": f(lru_wa), "lru_wx": f(lru_wx),
        "dftL": dftL, "dftC": dftC, "dftK": dftK, "dftM": dftM, "ident": np.eye(128, dtype=np.float32),
    }
    xTa = np.ascontiguousarray(x.transpose(0, 2, 1))
    cTa = np.ascontiguousarray(ctx.transpose(0, 2, 1))
    in_maps = []
    for core in range(NCORES):
        b0 = core * BPC
        rows = np.stack([c[b0], c[b0 + 1], c_ctx])
        cTm = np.ascontiguousarray(rows.reshape(3, KC, 128).transpose(2, 1, 0))
        m = dict(shared)
        m["xT"] = xTa[b0:b0 + BPC]
        m["ctxT"] = cTa[b0:b0 + BPC]
        m["cT"] = cTm
        in_maps.append(m)
    if "nc" not in _CACHE:
        _CACHE["nc"] = build_program()
    res = run_bass_kernel_spmd(_CACHE["nc"], in_maps, core_ids=list(range(NCORES)))
    outT = np.concatenate([np.asarray(r["outT"]) for r in res.results], axis=0)
    return np.ascontiguousarray(outT.transpose(0, 2, 1)).astype(np.float32)
```

```python
import numpy as np
from contextlib import ExitStack
import concourse.bass as bass
import concourse.mybir as mybir
from concourse.bass_utils import run_bass_kernel_spmd

F32 = mybir.dt.float32
BF16 = mybir.dt.bfloat16
AF = mybir.ActivationFunctionType
ALU = mybir.AluOpType

D = 1024
KC = 8
SEQ = 2048
CTX = 256
T = CTX + SEQ
DEPTH = 2
FH = 2816
OFF_LX, OFF_LG, OFF_C, OFF_G = 512, 1024, 1536, 2560
IN_W = 5632
NCORES = 8
BPC = 2

VEC = {}
_off = 0
for _n, _w in (("ada_b", 48), ("n1g", 8), ("n2g", 8), ("in_b", 44), ("lcw", 16), ("lcb", 4),
               ("lba", 8), ("lbx", 8), ("lam", 8), ("ccw", 124), ("ccb", 4), ("clg", 4),
               ("clb", 4), ("fcw", 132), ("fcb", 44), ("fing", 8)):
    VEC[_n] = _off
    _off += _w
NV = _off

SEG512 = [(0, 256, True)] + [(256 + 512 * i, 512, False) for i in range(4)]
SEG256 = [(0, 256, True)] + [(256 + 256 * i, 256, False) for i in range(8)]


class Reg:
    __slots__ = ("w", "r")

    def __init__(self):
        self.w = {}
        self.r = {}


class KB:
    def __init__(self, nc, es):
        self.nc = nc
        self.eng = {"pe": nc.tensor, "act": nc.scalar, "dve": nc.vector, "pool": nc.gpsimd, "sp": nc.sync}
        self.sem = {}
        self.cnt = {}
        self.waited = {e: {} for e in self.eng}
        for e in self.eng:
            self.sem[e] = es.enter_context(nc.semaphore("s_" + e))
            self.cnt[e] = 0
        self.R = 8
        for q in ("sp", "pool"):
            for i in range(self.R):
                k = ("dq", q, i)
                self.sem[k] = es.enter_context(nc.semaphore("d_%s%d" % (q, i)))
                self.cnt[k] = 0
        self.rr = {"sp": 0, "pool": 0}
        self.regs = {}
        self.nins = 0
        self.ghost = Reg()

    def reg(self, *key):
        r = self.regs.get(key)
        if r is None:
            r = self.regs[key] = Reg()
            r.w = dict(self.ghost.w)
            r.r = dict(self.ghost.r)
        return r

    def phase_end(self, keep=("H", "NT", "ps", "consts", "YF", "YR", "YC", "dftLb")):
        g = self.ghost
        for key, r in list(self.regs.items()):
            for k, v in r.w.items():
                if g.w.get(k, 0) < v:
                    g.w[k] = v
            for k, v in r.r.items():
                if g.r.get(k, 0) < v:
                    g.r[k] = v
            if key[0] not in keep:
                del self.regs[key]

    def _need(self, E, k, v, waits):
        if self.waited[E].get(k, 0) >= v:
            return
        if waits.get(k, 0) < v:
            waits[k] = v

    def _emit_waits(self, E, waits):
        for k, v in waits.items():
            self.eng[E].wait_ge(self.sem[k], v)
            self.waited[E][k] = v
            self.nins += 1

    def _deps(self, E, reads, writes, ident, waits):
        for r in reads:
            for k, v in r.w.items():
                if k == ident and E == "pe":
                    continue
                self._need(E, k, v, waits)
        for w in writes:
            for k, v in w.w.items():
                if k == ident and E == "pe":
                    continue
                self._need(E, k, v, waits)
            for k, v in w.r.items():
                if k == ident:
                    continue
                self._need(E, k, v, waits)

    def op(self, E, fn, reads=(), writes=()):
        waits = {}
        self._deps(E, reads, writes, E, waits)
        self._emit_waits(E, waits)
        ins = fn(self.eng[E])
        self.cnt[E] += 1
        c = self.cnt[E]
        ins.then_inc(self.sem[E], 1)
        self.nins += 1
        for r in reads:
            r.r[E] = c
        for w in writes:
            w.w = {E: c}
            w.r = {}

    def dma(self, Q, out, in_, reads=(), writes=()):
        i = self.rr[Q] % self.R
        self.rr[Q] += 1
        k = ("dq", Q, i)
        waits = {}
        if self.cnt[k] > 0:
            self._need(Q, k, self.cnt[k], waits)
        self._deps(Q, reads, writes, None, waits)
        self._emit_waits(Q, waits)
        ins = self.eng[Q].dma_start(out=out, in_=in_)
        self.cnt[k] += 16
        c = self.cnt[k]
        ins.then_inc(self.sem[k], 16)
        self.nins += 1
        for r in reads:
            r.r[k] = c
        for w in writes:
            w.w = {k: c}
            w.r = {}

    def barrier(self):
        for E in self.eng:
            waits = {}
            for k, c in self.cnt.items():
                if k != E and c > 0:
                    self._need(E, k, c, waits)
            self._emit_waits(E, waits)
        self.regs.clear()


class Rot:
    uid = 0

    def __init__(self, kb, es, name, shape, dtype, n):
        self.kb = kb
        self.name = name
        Rot.uid += 1
        self.t = [es.enter_context(kb.nc.sbuf_tensor("%s_r%d_%d" % (name, Rot.uid, i), shape, dtype)) for i in range(n)]
        self.i = 0

    def get(self):
        i = self.i % len(self.t)
        self.i += 1
        return self.t[i], self.kb.reg(self.name, i)


class ViewRot:
    def __init__(self, kb, name, views):
        self.kb = kb
        self.name = name
        self.t = list(views)
        self.i = 0

    def get(self):
        i = self.i % len(self.t)
        self.i += 1
        return self.t[i], self.kb.reg(self.name, i)


def build_program():
    nc = bass.Bass("TRN2", target_bir_lowering=False)
    dt = nc.dram_tensor
    xT = dt("xT", [BPC, D, SEQ], F32, kind="ExternalInput").ap()
    ctxT = dt("ctxT", [BPC, D, CTX], F32, kind="ExternalInput").ap()
    cT = dt("cT", [128, KC, 3], F32, kind="ExternalInput").ap()
    vecs_d = dt("vecs", [128, DEPTH, NV], F32, kind="ExternalInput").ap()
    inbf_d = dt("inbf", [DEPTH, 1, 512], F32, kind="ExternalInput").ap()
    ada_w = dt("ada_w", [DEPTH, D, 6 * D], F32, kind="ExternalInput").ap()
    in_w = dt("in_w", [DEPTH, D, IN_W], F32, kind="ExternalInput").ap()
    fo_w = dt("fourier_out_w", [DEPTH, 512, D], F32, kind="ExternalInput").ap()
    lo_w = dt("lru_out_w", [DEPTH, 512, D], F32, kind="ExternalInput").ap()
    co_w = dt("conf_out_w", [DEPTH, 512, D], F32, kind="ExternalInput").ap()
    mix_w = dt("mix_out_w", [DEPTH, D, D], F32, kind="ExternalInput").ap()
    up_w = dt("ffn_up_w", [DEPTH, D, 2 * FH], F32, kind="ExternalInput").ap()
    dn_w = dt("ffn_down_w", [DEPTH, FH, D], F32, kind="ExternalInput").ap()
    lwa = dt("lru_wa", [DEPTH, 2, 8, 64, 64], F32, kind="ExternalInput").ap()
    lwx = dt("lru_wx", [DEPTH, 2, 8, 64, 64], F32, kind="ExternalInput").ap()
    dftL = dt("dftL", [2, 4, 128, 16 * 256], F32, kind="ExternalInput").ap()
    dftM = dt("dftM", [2, 128, 32], F32, kind="ExternalInput").ap()
    dftC = dt("dftC", [2, 128, 2 * 256], F32, kind="ExternalInput").ap()
    dftK = dt("dftK", [3, 128, 128], F32, kind="ExternalInput").ap()
    ident_d = dt("ident", [128, 128], F32, kind="ExternalInput").ap()
    outT = dt("outT", [BPC, D, SEQ], F32, kind="ExternalOutput").ap()
    dftLb = dt("dftLb", [2, 4, 128, 16 * 256], BF16, kind="Internal").ap()

    with ExitStack() as es:
        es.enter_context(nc.allow_low_precision("bf16 matmul operands, fp32 accumulation"))
        kb = KB(nc, es)
        uid = [0]

        def sb(name, shape, dtype, st=es):
            uid[0] += 1
            return st.enter_context(nc.sbuf_tensor("%s_u%d" % (name, uid[0]), shape, dtype))

        NT = sb("NT", [128, KC, T], BF16)
        BREG = sb("BREG", [128, 3 * 4 * T], BF16)
        YF = BREG[:, 0:4 * T].rearrange("p (c t) -> p c t", c=4)
        YR = BREG[:, 4 * T:8 * T].rearrange("p (c t) -> p c t", c=4)
        YC = BREG[:, 8 * T:12 * T].rearrange("p (c t) -> p c t", c=4)
        hsp = dt("hsp", [KC, 128, T], F32, kind="Internal").ap()
        KEEP = ("NT", "ps", "consts", "consts2", "dftLb", "hsp")
        VECS = sb("VECS", [128, DEPTH, NV], F32)
        MOD = sb("MOD", [128, DEPTH, 48, 3], F32)
        GS = sb("GS", [128, DEPTH, 2, KC, 3], F32)
        CONST = sb("CONST", [128, 4], F32)
        IDB = sb("IDB", [128, 128], BF16)
        ONES = sb("ONES", [128, 128], BF16)
        CKS = sb("CKS", [128, 3, 128], BF16)
        SCT = sb("SCT", [128, KC, 3], BF16)
        PS = [es.enter_context(nc.psum_tensor("ps%d" % i, [128, 512], F32)) for i in range(8)]
        bank_i = [0]

        def bank():
            i = bank_i[0] % 8
            bank_i[0] += 1
            return PS[i], kb.reg("ps", i)

        rH = lambda c, s: kb.reg("H", c, s)
        rNT = lambda c, s: kb.reg("NT", c, s)
        rC = kb.reg("consts")

        def vec(l, name, idx=0):
            o = VEC[name] + idx
            return VECS[:, l, o:o + 1]

        def mod(l, which, kc, r):
            return MOD[:, l, which * 8 + kc, r:r + 1]

        evac_flip = [0]

        def copy_any(out, in_, reads, writes):
            evac_flip[0] ^= 1
            if evac_flip[0]:
                kb.op("act", lambda e: e.activation(out=out, in_=in_, func=AF.Identity), reads, writes)
            else:
                kb.op("dve", lambda e: e.tensor_copy(out=out, in_=in_), reads, writes)

        def mm(ps_ap, lhsT, rhs, start, stop, reads, writes):
            kb.op("pe", lambda e: e.matmul(ps_ap, lhsT, rhs, start=start, stop=stop), reads, writes)

        kb.dma("sp", VECS[:], vecs_d, (), (rC,))
        kb.dma("pool", IDB[:], ident_d, (), (rC,))
        kb.dma("pool", CKS[:, 0, :], dftK[0], (), (rC,))
        kb.dma("pool", CKS[:, 1, :], dftK[1], (), (rC,))
        kb.dma("pool", CKS[:, 2, :], dftK[2], (), (rC,))
        kb.op("dve", lambda e: e.memset(ONES[:], 1.0), (), (rC,))
        kb.op("dve", lambda e: e.memset(CONST[:, 0:1], 1e-6), (), (rC,))
        kb.op("dve", lambda e: e.memset(CONST[:, 1:2], 1e-5), (), (rC,))
        kb.op("dve", lambda e: e.memset(CONST[:, 2:3], 1.0), (), (rC,))
        kb.op("dve", lambda e: e.memset(CONST[:, 3:4], 0.0), (), (rC,))
        EPS_RMS, EPS_LN, ONE1 = CONST[:, 0:1], CONST[:, 1:2], CONST[:, 2:3]
        rdl = kb.reg("dftLb")
        rC2 = kb.reg("consts2")

        def ada_gen(l, WAr, rcm, bankfn):
            aw = ada_w[l].rearrange("(kc p) n -> p kc n", p=128)

            def load(cg_):
                wt_, wr_ = WAr.get()
                kb.dma("pool", wt_[:], aw[:, :, cg_ * 512:(cg_ + 1) * 512], (), (wr_,))
                return wt_, wr_

            nxt_ = load(0)
            for cg in range(12):
                wt, wr = nxt_
                if cg < 11:
                    nxt_ = load(cg + 1)
                ps, pr = bankfn()
                for f in range(4):
                    for kc in range(KC):
                        mm(ps[:, f * 3:(f + 1) * 3], wt[:, kc, f * 128:(f + 1) * 128], SCT[:, kc, :],
                           kc == 0, kc == KC - 1, (wr, rC), (pr,))
                for r in range(3):
                    o = VEC["ada_b"] + cg * 4
                    kb.op("dve", lambda e, r=r, ps=ps, o=o, cg=cg: e.tensor_tensor(
                        out=MOD[:, l, cg * 4:(cg + 1) * 4, r],
                        in0=ps[:, 0:12].rearrange("p (f r) -> p f r", r=3)[:, :, r],
                        in1=VECS[:, l, o:o + 4], op=ALU.add), (pr, rC), (rcm,))
                yield
            for n_i, (gname, which) in enumerate((("n1g", 1), ("n2g", 4))):
                for r in range(3):
                    kb.op("dve", lambda e, n_i=n_i, which=which, r=r: e.tensor_scalar(
                        out=GS[:, l, n_i, :, r], in0=MOD[:, l, which * 8:(which + 1) * 8, r],
                        scalar1=1.0, scalar2=None, op0=ALU.add), (rcm,), (rcm,))
                    o = VEC[gname]
                    kb.op("dve", lambda e, n_i=n_i, r=r, o=o: e.tensor_tensor(
                        out=GS[:, l, n_i, :, r], in0=GS[:, l, n_i, :, r], in1=VECS[:, l, o:o + 8],
                        op=ALU.mult), (rcm, rC), (rcm,))

        with ExitStack() as p0:
            CTs = sb("CTs", [128, KC, 3], F32, p0)
            rct = kb.reg("CTs")
            kb.dma("sp", CTs[:], cT, (), (rct,))
            kb.op("act", lambda e: e.activation(out=SCT[:], in_=CTs[:], func=AF.Silu), (rct,), (rC,))
            WA = Rot(kb, p0, "adaW", [128, KC, 512], BF16, 2)
            DS = Rot(kb, p0, "dstage", [128, 4096], F32, 3)
            DB = Rot(kb, p0, "dstageb", [128, 4096], BF16, 2)
            chunks = [(half, mb) for mb in range(4) for half in range(2)]
            staged = []
            for ci in range(len(chunks) + 2):
                if ci < len(chunks):
                    ds, dsr = DS.get()
                    kb.dma("sp", ds[:], dftL[chunks[ci][0], chunks[ci][1]], (), (dsr,))
                    staged.append((ds, dsr))
                if ci >= 2:
                    half, mb = chunks[ci - 2]
                    ds, dsr = staged[ci - 2]
                    db, dbr = DB.get()
                    copy_any(db[:], ds[:], (dsr,), (dbr,))
                    kb.dma("sp", dftLb[half, mb], db[:], (dbr,), (rdl,))
            for _ in ada_gen(0, WA, rC, bank):
                pass
            kb.barrier()

        def run_rr(gens):
            gens = list(gens)
            while gens:
                for g_ in list(gens):
                    try:
                        next(g_)
                    except StopIteration:
                        gens.remove(g_)

        def rr_gen(gens):
            gens = list(gens)
            while gens:
                for g_ in list(gens):
                    try:
                        next(g_)
                    except StopIteration:
                        gens.remove(g_)
                yield

        bankL_i = [0]
        bankC_i = [0]

        def bankL():
            i = bankL_i[0] % 4
            bankL_i[0] += 1
            return PS[i], kb.reg("ps", i)

        def bankC():
            i = 4 + bankC_i[0] % 4
            bankC_i[0] += 1
            return PS[i], kb.reg("ps", i)

        def rms_mod(l, n_i, sh_which, bi, segs, tmp, out_fn, extra=(), nslots=2):
            SQ, RS, TM = tmp

            def blocks(slot):
                sq, sqr = SQ.t[slot], kb.reg(SQ.name, slot)
                rs, rsr = RS.t[slot], kb.reg(RS.name, slot)
                tm, tmr = TM.t[slot], kb.reg(TM.name, slot)
                for (s, N, isctx) in segs[slot::nslots]:
                    si = 0 if isctx else 1 + (s - CTX) // 512
                    r = 2 if isctx else bi
                    ps, pr = PS[8 - nslots + slot], kb.reg("ps", 8 - nslots + slot)
                    for kc in range(KC):
                        kb.op("act", lambda e: e.activation(
                            out=sq[:, 0:N], in_=H[:, kc, s:s + N], func=AF.Square), (rH(kc, si),), (sqr,))
                        mm(ps[:, 0:N], ONES[:], sq[:, 0:N], kc == 0, kc == KC - 1, (sqr, rC), (pr,))
                        yield
                    kb.op("act", lambda e: e.activation(
                        out=rs[:, 0:N], in_=ps[:, 0:N], func=AF.Ln, scale=1.0 / D, bias=EPS_RMS), (pr, rC), (rsr,))
                    yield
                    kb.op("act", lambda e: e.activation(
                        out=rs[:, 0:N], in_=rs[:, 0:N], func=AF.Exp, scale=-0.5), (rsr,), (rsr,))
                    yield
                    for kc in range(KC):
                        tmk, tmkr = tm, tmr
                        kb.op("dve", lambda e: e.tensor_tensor(
                            out=tmk[:, 0:N], in0=H[:, kc, s:s + N], in1=rs[:, 0:N], op=ALU.mult),
                            (rH(kc, si), rsr), (tmkr,))
                        yield
                        o_ap, o_r = out_fn(kc, si, s, N)
                        if kc % 4 == 3:
                            kb.op("dve", lambda e: e.tensor_scalar(
                                out=o_ap, in0=tmk[:, 0:N], scalar1=GS[:, l, n_i, kc, r:r + 1],
                                scalar2=mod(l, sh_which, kc, r), op0=ALU.mult, op1=ALU.add), (tmkr, rC, rC2), (o_r,))
                        else:
                            kb.op("act", lambda e: e.activation(
                                out=o_ap, in_=tmk[:, 0:N], func=AF.Identity, scale=GS[:, l, n_i, kc, r:r + 1],
                                bias=mod(l, sh_which, kc, r)), (tmkr, rC, rC2), (o_r,))
                        yield

            run_rr([blocks(i_) for i_ in range(nslots)] + list(extra))

        def nt_out(kc, si, s, N):
            return NT[:, kc, s:s + N], rNT(kc, si)

        def padviews(Pc, Pl, s, N, isctx, lpad):
            if isctx:
                return (Pc[:, lpad:lpad + N], lambda k: Pc[:, k:k + N], lambda ps: ps[:, 0:N])
            r0 = (s - CTX) // 64
            nr = N // 64
            return (Pl[:, r0:r0 + nr, lpad:lpad + 64], lambda k: Pl[:, r0:r0 + nr, k:k + 64],
                    lambda ps: ps[:, 0:N].rearrange("p (r c) -> p r c", c=64))

        for bi in range(BPC):
            hst = ExitStack()
            H = sb("H", [128, KC, T], F32, hst)

            def load_h_from_inputs():
                for kc in range(KC):
                    kb.dma("sp", H[:, kc, 0:CTX], ctxT[bi, kc * 128:(kc + 1) * 128, :], (), (rH(kc, 0),))
                    for q in range(4):
                        kb.dma("sp", H[:, kc, CTX + q * 512:CTX + (q + 1) * 512],
                               xT[bi, kc * 128:(kc + 1) * 128, q * 512:(q + 1) * 512], (), (rH(kc, 1 + q),))

            load_h_from_inputs()
            for l in range(DEPTH):
                last = l == DEPTH - 1
                inw = in_w[l].rearrange("(kc p) n -> p kc n", p=128)
                segs_loc = [sg for sg in SEG512 if not (last and sg[2])]
                seg_idx = {sg: i for i, sg in enumerate(SEG512)}

                with ExitStack() as phn:
                    tmp = (Rot(kb, phn, "sq", [128, 512], BF16, 3), Rot(kb, phn, "rs", [128, 512], F32, 3),
                           Rot(kb, phn, "tm", [128, 512], F32, 3))
                    extra = []
                    if bi == 0 and l == 0:
                        WA1 = Rot(kb, phn, "adaW1", [128, KC, 512], BF16, 2)
                        extra = [ada_gen(1, WA1, rC2, bankL)]
                    rms_mod(l, 0, 0, bi, SEG512, tmp, nt_out, extra=extra, nslots=3)
                if l > 0:
                    for kc in range(KC):
                        kb.dma("sp", hsp[kc], H[:, kc, :], tuple(rH(kc, i) for i in range(5)), (kb.reg("hsp", kc),))
                kb.phase_end(KEEP)
                hst.close()

                with ExitStack() as ph:
                    UFT = sb("UFT", [128, 18, 512], BF16, ph)
                    WF = sb("WF", [128, KC, 512], BF16, ph)
                    BFR = sb("BFR", [1, 512], BF16, ph)
                    DCt = sb("DCt", [128, 2, 2, 256], BF16, ph)
                    DL = Rot(kb, ph, "DL", [128, 2, 16 * 256], BF16, 2)
                    AB = Rot(kb, ph, "AB", [128, 512], BF16, 3)
                    rW = kb.reg("WF")
                    kb.dma("pool", WF[:], inw[:, :, 0:512], (), (rW,))
                    kb.dma("pool", BFR[:], inbf_d[l], (), (rW,))
                    kb.dma("pool", DCt[:, 0, :, :].rearrange("p a b -> p (a b)"), dftC[0], (), (rW,))
                    kb.dma("pool", DCt[:, 1, :, :].rearrange("p a b -> p (a b)"), dftC[1], (), (rW,))
                    DMt = sb("DMt", [128, 2, 16, 2], BF16, ph)
                    kb.dma("pool", DMt[:, 0, :, :].rearrange("p a b -> p (a b)"), dftM[0], (), (rW,))
                    kb.dma("pool", DMt[:, 1, :, :].rearrange("p a b -> p (a b)"), dftM[1], (), (rW,))
                    tcs = list(range(18)) if not last else list(range(2, 18))
                    for tc in tcs:
                        si = 0 if tc < 2 else 1 + (tc - 2) // 4
                        ps, pr = bank()
                        for kc in range(KC):
                            mm(ps[:], NT[:, kc, tc * 128:(tc + 1) * 128], WF[:, kc, :], kc == 0, False,
                               (rNT(kc, si), rW), (pr,))
                        mm(ps[:], ONES[0:1, :], BFR[0:1, :], False, True, (rC, rW), (pr,))
                        copy_any(UFT[:, tc, :], ps[:], (pr,), (kb.reg("UFT", tc),))
                    uft_regs = [kb.reg("UFT", tc) for tc in tcs]

                    def chan_dft(ab, abr, M, out_ap, out_r, mirror=None):
                        ps2, pr2 = bank()
                        mm(ps2[:, 0:M], CKS[:, 0, :], ab[:, 0:M], True, False, (abr, rC), (pr2,))
                        mm(ps2[:, 0:M], CKS[:, 1, :], ab[:, 256:256 + M], False, True, (abr, rC), (pr2,))
                        copy_any(out_ap, ps2[:, 0:M], (pr2,), (out_r,))
                        if mirror is not None:
                            m0, m_ap, m_r = mirror
                            ps3, pr3 = bank()
                            mm(ps3[:, 0:M], CKS[:, 0, :], ab[:, 0:M], True, False, (abr, rC), (pr3,))
                            mm(ps3[:, 0:M], CKS[:, 2, :], ab[:, 256:256 + M], False, True, (abr, rC), (pr3,))
                            kb.op("dve", lambda e: e.tensor_copy(out=m_ap, in_=ps3[:, m0:M]), (pr3,), (m_r,))

                    if not last:
                        for g in range(4):
                            ps, pr = bank()
                            for half in range(2):
                                for lc in range(2):
                                    mm(ps[:, half * 256:(half + 1) * 256], UFT[:, lc, g * 128:(g + 1) * 128],
                                       DCt[:, half, lc, :], lc == 0, lc == 1, (uft_regs[lc], rW), (pr,))
                            ab, abr = AB.get()
                            copy_any(ab[:], ps[:], (pr,), (abr,))
                            chan_dft(ab, abr, 256, YF[:, g, 0:256], kb.reg("YF", g, 0))
                    def load_dl(mb):
                        dl_, dlr_ = DL.get()
                        kb.dma("sp", dl_[:, 0, :], dftLb[0, mb], (rdl,), (dlr_,))
                        kb.dma("sp", dl_[:, 1, :], dftLb[1, mb], (rdl,), (dlr_,))
                        return dl_, dlr_

                    nxt_dl = load_dl(0)
                    pend_cd = None
                    for mb in range(4):
                        dl, dlr = nxt_dl
                        if mb < 3:
                            nxt_dl = load_dl(mb + 1)
                        for g in range(4):
                            ps, pr = bank()
                            for half in range(2):
                                for lc in range(16):
                                    mm(ps[:, half * 256:(half + 1) * 256],
                                       UFT[:, 2 + lc, g * 128:(g + 1) * 128],
                                       dl[:, half, lc * 256:(lc + 1) * 256], lc == 0, lc == 15,
                                       (kb.reg("UFT", 2 + lc), dlr), (pr,))
                            ab, abr = AB.get()
                            copy_any(ab[:], ps[:], (pr,), (abr,))
                            if pend_cd is not None:
                                chan_dft(*pend_cd)
                            m0 = 1 if mb == 0 else 0
                            pend_cd = (ab, abr, 256, YF[:, g, CTX + mb * 256:CTX + (mb + 1) * 256],
                                       kb.reg("YF", g, 1 + mb),
                                       (m0, YF[:, g, CTX + 1793 - mb * 256:CTX + 2049 - mb * 256 - m0][:, ::-1],
                                        kb.reg("YF", g, 5 + mb)))
                    chan_dft(*pend_cd)
                    for g in range(4):
                        ps, pr = bank()
                        for half in range(2):
                            for lc in range(16):
                                mm(ps[:, half * 256:half * 256 + 2], UFT[:, 2 + lc, g * 128:(g + 1) * 128],
                                   DMt[:, half, lc, :], lc == 0, lc == 15, (kb.reg("UFT", 2 + lc), rW), (pr,))
                        ab, abr = AB.get()
                        kb.op("dve", lambda e, ab=ab, ps=ps: e.tensor_copy(
                            out=ab[:].rearrange("p (h c) -> p h c", h=2)[:, :, 0:2],
                            in_=ps[:].rearrange("p (h c) -> p h c", h=2)[:, :, 0:2]), (pr,), (abr,))
                        ps2, pr2 = bank()
                        mm(ps2[:, 0:2], CKS[:, 0, :], ab[:, 0:2], True, False, (abr, rC), (pr2,))
                        mm(ps2[:, 0:2], CKS[:, 1, :], ab[:, 256:258], False, True, (abr, rC), (pr2,))
                        kb.op("dve", lambda e, ps2=ps2, g=g: e.tensor_copy(
                            out=YF[:, g, CTX + 1024:CTX + 1025], in_=ps2[:, 0:1]), (pr2,), (kb.reg("YF", g, 9),))
                    kb.phase_end(KEEP)

                with ExitStack() as ph:
                    WL = Rot(kb, ph, "WL", [128, 2, KC, 128], BF16, 2)
                    UXc = sb("UXc", [128, CTX + 3], BF16, ph)
                    UXl = sb("UXl", [128, 32, 67], BF16, ph)
                    XC = sb("XC", [128, T], F32, ph)
                    XCB = sb("XCB", [128, T], BF16, ph)
                    HF = sb("HF", [128, T], F32, ph)
                    HB = sb("HB", [128, T], F32, ph)
                    WG = sb("WG", [128, 2, 2, 128], BF16, ph)
                    DGL = sb("DGL", [128, 4, 128], BF16, ph)
                    SA = sb("SA", [128, 3, 8], F32, ph)
                    NB = sb("NB", [128, 16], F32, ph)
                    TP = Rot(kb, ph, "lt", [128, 512], F32, 12)
                    TC = Rot(kb, ph, "lc", [128, 256], F32, 2)
                    rux = kb.reg("UX")
                    rwg = kb.reg("WG")
                    rsa = kb.reg("SA")
                    kb.op("dve", lambda e: e.memset(UXc[:], 0.0), (), (rux,))
                    kb.op("dve", lambda e: e.memset(UXl[:], 0.0), (), (rux,))
                    kb.op("dve", lambda e: e.memset(WG[:], 0.0), (), (rwg,))
                    o = VEC["lam"]
                    kb.op("act", lambda e: e.activation(out=SA[:, 0, :], in_=VECS[:, l, o:o + 8], func=AF.Exp,
                                                        scale=-1.0), (rC,), (rsa,))
                    kb.op("act", lambda e: e.activation(out=SA[:, 0, :], in_=SA[:, 0, :], func=AF.Ln,
                                                        bias=ONE1), (rsa, rC), (rsa,))
                    kb.op("dve", lambda e: e.tensor_scalar(out=SA[:, 1, :], in0=SA[:, 0, :], scalar1=-8.0,
                                                           scalar2=None, op0=ALU.mult), (rsa,), (rsa,))
                    kb.op("dve", lambda e: e.tensor_scalar(out=SA[:, 2, :], in0=SA[:, 0, :], scalar1=-16.0,
                                                           scalar2=None, op0=ALU.mult), (rsa,), (rsa,))
                    o2 = VEC["lba"]
                    kb.op("dve", lambda e: e.tensor_scalar(out=NB[:, :], in0=VECS[:, l, o2:o2 + 16], scalar1=-1.0,
                                                           scalar2=None, op0=ALU.mult), (rC,), (rsa,))
                    WC = Rot(kb, ph, "WC", [128, 2, KC, 128], BF16, 2)
                    DGC = sb("DGC", [128, 31, 128], BF16, ph)
                    VPc = sb("VPc", [128, CTX + 30], BF16, ph)
                    VPl = sb("VPl", [128, 32, 94], BF16, ph)
                    TS = Rot(kb, ph, "ct", [128, 512], F32, 2)
                    TL = Rot(kb, ph, "cl", [128, 256], F32, 6)
                    SQ = Rot(kb, ph, "csq", [128, 256], BF16, 2)
                    rvp = kb.reg("VP")
                    NBC = sb("NBC", [128, 4], F32, ph)
                    o3 = VEC["in_b"] + 16
                    kb.op("dve", lambda e: e.tensor_scalar(out=NBC[:, :], in0=VECS[:, l, o3:o3 + 4], scalar1=-1.0,
                                                           scalar2=None, op0=ALU.mult), (rC,), (rC,))
                    kb.op("dve", lambda e: e.memset(VPc[:], 0.0), (), (rvp,))
                    kb.op("dve", lambda e: e.memset(VPl[:], 0.0), (), (rvp,))

                    def lru_gen():
                        def load_wl(c_):
                            wl_, wlr_ = WL.get()
                            kb.dma("pool", wl_[:, 0, :, :], inw[:, :, OFF_LX + c_ * 128:OFF_LX + (c_ + 1) * 128], (), (wlr_,))
                            kb.dma("pool", wl_[:, 1, :, :], inw[:, :, OFF_LG + c_ * 128:OFF_LG + (c_ + 1) * 128], (), (wlr_,))
                            return wl_, wlr_

                        nxt_wl = load_wl(0)
                        for c in range(4):
                            wl, wlr = nxt_wl
                            if c < 3:
                                nxt_wl = load_wl(c + 1)
                            for d in range(2):
                                for gi, wsrc in enumerate((lwa, lwx)):
                                    for hh in range(2):
                                        kb.dma("pool", WG[hh * 64:(hh + 1) * 64, d, gi, hh * 64:(hh + 1) * 64],
                                               wsrc[l, d, 2 * c + hh], (), (rwg,))
                            for k in range(4):
                                kb.op("dve", lambda e, k=k, c=c: e.tensor_scalar(
                                    out=DGL[:, k, :], in0=IDB[:], scalar1=vec(l, "lcw", c * 4 + k), scalar2=None,
                                    op0=ALU.mult), (rC,), (kb.reg("DGL"),))
                            for si, (s, N, isctx) in enumerate(SEG512):
                                interior, tap, psv = padviews(UXc, UXl, s, N, isctx, 2)
                                ps, pr = bankL()
                                for kc in range(KC):
                                    mm(ps[:, 0:N], wl[:, 0, kc, :], NT[:, kc, s:s + N], kc == 0, kc == KC - 1,
                                       (wlr, rNT(kc, si)), (pr,))
                                kb.op("dve", lambda e, interior=interior, psv=psv, ps=ps, c=c: e.tensor_scalar(
                                    out=interior, in0=psv(ps), scalar1=vec(l, "in_b", 4 + c), scalar2=None, op0=ALU.add),
                                    (pr, rC), (kb.reg("UXs", si),))
                                yield
                            gl = []
                            for si, (s, N, isctx) in enumerate(SEG512):
                                if last and isctx:
                                    continue
                                ps2, pr2 = bankL()
                                for kc in range(KC):
                                    mm(ps2[:, 0:N], wl[:, 1, kc, :], NT[:, kc, s:s + N], kc == 0, kc == KC - 1,
                                       (wlr, rNT(kc, si)), (pr2,))
                                gl.append((ps2, pr2, si, s, N))
                                if len(gl) == 4 or si == 4:
                                    for (ps2, pr2, si_, s_, N_) in gl:
                                        kb.op("act", lambda e, ps2=ps2, c=c, s_=s_, N_=N_: e.activation(
                                            out=YR[:, c, s_:s_ + N_], in_=ps2[:, 0:N_], func=AF.Gelu, bias=vec(l, "in_b", 8 + c)),
                                            (pr2, rC), (kb.reg("YR", c, si_),))
                                    gl = []
                            yield
                            for si, (s, N, isctx) in enumerate(SEG512):
                                interior, tap, psv = padviews(UXc, UXl, s, N, isctx, 2)
                                ps, pr = bankL()
                                for k in range(4):
                                    mm(psv(ps), DGL[:, k, :], tap(k), k == 0, k == 3,
                                       (kb.reg("DGL"), kb.reg("UXs", si), rux), (pr,))
                                kb.op("dve", lambda e, ps=ps, s=s, N=N, c=c: e.tensor_scalar(
                                    out=XC[:, s:s + N], in0=ps[:, 0:N], scalar1=vec(l, "lcb", c), scalar2=None, op0=ALU.add),
                                    (pr, rC), (kb.reg("XC", si),))
                                kb.op("dve", lambda e, ps=ps, s=s, N=N, c=c: e.tensor_scalar(
                                    out=XCB[:, s:s + N], in0=ps[:, 0:N], scalar1=vec(l, "lcb", c), scalar2=None, op0=ALU.add),
                                    (pr, rC), (kb.reg("XCB", si),))
                                yield
                            orders = {0: [0, 1, 2, 3, 4], 1: [0, 4, 3, 2, 1]}
                            ready = {0: {}, 1: {}}
                            scanned = {0: 0, 1: 0}

                            def gates(d, par, c=c):
                                q = 2 * d + par
                                tr, trr = TP.t[3 * q], kb.reg("lt", 3 * q)
                                ti, tir = TP.t[3 * q + 1], kb.reg("lt", 3 * q + 1)
                                ta, tar = TP.t[3 * q + 2], kb.reg("lt", 3 * q + 2)
                                psb, prb = PS[q], kb.reg("ps", q)
                                for oi in range(par, 5, 2):
                                    si = orders[d][oi]
                                    while scanned[d] < oi - 1:
                                        yield
                                    s, N, isctx = SEG512[si]
                                    mm(psb[:, 0:N], WG[:, d, 0, :], XCB[:, s:s + N], True, True, (rwg, kb.reg("XCB", si)), (prb,))
                                    yield
                                    kb.op("act", lambda e: e.activation(
                                        out=tr[:, 0:N], in_=psb[:, 0:N], func=AF.Exp, scale=-1.0, bias=NB[:, d * 4 + c:d * 4 + c + 1]),
                                        (prb, rsa), (trr,))
                                    mm(psb[:, 0:N], WG[:, d, 1, :], XCB[:, s:s + N], True, True, (rwg, kb.reg("XCB", si)), (prb,))
                                    yield
                                    kb.op("act", lambda e: e.activation(
                                        out=ti[:, 0:N], in_=psb[:, 0:N], func=AF.Exp, scale=-1.0, bias=NB[:, 8 + d * 4 + c:8 + d * 4 + c + 1]),
                                        (prb, rsa), (tir,))
                                    yield
                                    kb.op("act", lambda e: e.activation(
                                        out=tr[:, 0:N], in_=tr[:, 0:N], func=AF.Ln, bias=ONE1), (trr, rC), (trr,))
                                    yield
                                    kb.op("act", lambda e: e.activation(
                                        out=ti[:, 0:N], in_=ti[:, 0:N], func=AF.Ln, bias=ONE1), (tir, rC), (tir,))
                                    yield
                                    kb.op("act", lambda e: e.activation(
                                        out=tr[:, 0:N], in_=tr[:, 0:N], func=AF.Exp, scale=-1.0), (trr,), (trr,))
                                    yield
                                    kb.op("act", lambda e: e.activation(
                                        out=ti[:, 0:N], in_=ti[:, 0:N], func=AF.Exp, scale=-1.0), (tir,), (tir,))
                                    yield
                                    kb.op("act", lambda e: e.activation(
                                        out=ta[:, 0:N], in_=tr[:, 0:N], func=AF.Exp, scale=SA[:, 1, d * 4 + c:d * 4 + c + 1]),
                                        (trr, rsa), (tar,))
                                    kb.op("dve", lambda e: e.tensor_tensor(
                                        out=ti[:, 0:N], in0=ti[:, 0:N], in1=XC[:, s:s + N], op=ALU.mult),
                                        (tir, kb.reg("XC", si)), (tir,))
                                    yield
                                    kb.op("dve", lambda e: e.tensor_tensor(
                                        out=tr[:, 0:N], in0=ta[:, 0:N], in1=ta[:, 0:N], op=ALU.mult), (tar,), (trr,))
                                    yield
                                    kb.op("act", lambda e: e.activation(
                                        out=tr[:, 0:N], in_=tr[:, 0:N], func=AF.Ln, scale=-1.0, bias=ONE1),
                                        (trr, rC), (trr,))
                                    yield
                                    kb.op("act", lambda e: e.activation(
                                        out=tr[:, 0:N], in_=tr[:, 0:N], func=AF.Exp, scale=0.5), (trr,), (trr,))
                                    yield
                                    kb.op("dve", lambda e: e.tensor_tensor(
                                        out=ti[:, 0:N], in0=ti[:, 0:N], in1=tr[:, 0:N], op=ALU.mult),
                                        (tir, trr), (tir,))
                                    ready[d][oi] = (ta, tar, ti, tir)
                                    yield

                            def scans(d):
                                HD = HF if d == 0 else HB
                                hname = "HF" if d == 0 else "HB"
                                prev = None
                                for oi, si in enumerate(orders[d]):
                                    while oi not in ready[d]:
                                        yield
                                    ta, tar, ti, tir = ready[d][oi]
                                    s, N, isctx = SEG512[si]
                                    init = 0.0 if prev is None else prev[0]
                                    rd = (tar, tir) + (() if prev is None else (prev[1],))
                                    if d == 0:
                                        kb.op("dve", lambda e: e.tensor_tensor_scan(
                                            out=HD[:, s:s + N], data0=ta[:, 0:N], data1=ti[:, 0:N], initial=init,
                                            op0=ALU.mult, op1=ALU.add), rd, (kb.reg(hname, si),))
                                        prev = (HD[:, s + N - 1:s + N], kb.reg(hname, si))
                                    else:
                                        kb.op("dve", lambda e: e.tensor_tensor_scan(
                                            out=HD[:, s:s + N][:, ::-1], data0=ta[:, 0:N][:, ::-1], data1=ti[:, 0:N][:, ::-1],
                                            initial=init, op0=ALU.mult, op1=ALU.add), rd, (kb.reg(hname, si),))
                                        prev = (HD[:, s:s + 1], kb.reg(hname, si))
                                    scanned[d] = oi + 1
                                    yield

                            yield from rr_gen([gates(0, 0), gates(1, 0), gates(0, 1), gates(1, 1), scans(0), scans(1)])
                            for si, (s, N, isctx) in enumerate(SEG512):
                                if last and isctx:
                                    continue
                                for h0 in range(0, N, 256):
                                    tcb, tcr = TC.get()
                                    kb.op("pool", lambda e, tcb=tcb, s=s, h0=h0: e.tensor_tensor(
                                        out=tcb[:, 0:256], in0=HB[:, s + h0:s + h0 + 256], in1=HF[:, s + h0:s + h0 + 256], op=ALU.add),
                                        (kb.reg("HB", si), kb.reg("HF", si)), (tcr,))
                                    kb.op("pool", lambda e, tcb=tcb, s=s, h0=h0, c=c: e.tensor_tensor(
                                        out=YR[:, c, s + h0:s + h0 + 256], in0=tcb[:, 0:256], in1=YR[:, c, s + h0:s + h0 + 256], op=ALU.mult),
                                        (tcr, kb.reg("YR", c, si)), (kb.reg("YR", c, si),))
                                    yield

                    def conf_gen():
                        def load_wc(c_):
                            wc_, wcr_ = WC.get()
                            kb.dma("pool", wc_[:, 0, :, :], inw[:, :, OFF_C + c_ * 128:OFF_C + (c_ + 1) * 128], (), (wcr_,))
                            kb.dma("pool", wc_[:, 1, :, :], inw[:, :, OFF_C + 512 + c_ * 128:OFF_C + 512 + (c_ + 1) * 128], (), (wcr_,))
                            return wc_, wcr_

                        nxt_wc = load_wc(0)
                        for c in range(4):
                            wc, wcr = nxt_wc
                            if c < 3:
                                nxt_wc = load_wc(c + 1)
                            for k in range(31):
                                kb.op("dve", lambda e, k=k, c=c: e.tensor_scalar(
                                    out=DGC[:, k, :], in0=IDB[:], scalar1=vec(l, "ccw", c * 31 + k), scalar2=None,
                                    op0=ALU.mult), (rC,), (kb.reg("DGC", k),))
                                if k % 8 == 7:
                                    yield
                            for (s, N, isctx) in segs_loc:
                                si = seg_idx[(s, N, isctx)]
                                interior, tap, psv = padviews(VPc, VPl, s, N, isctx, 15)
                                ps_a, pr_a = bankC()
                                for kc in range(KC):
                                    mm(ps_a[:, 0:N], wc[:, 0, kc, :], NT[:, kc, s:s + N], kc == 0, kc == KC - 1,
                                       (wcr, rNT(kc, si)), (pr_a,))
                                ps_g, pr_g = bankC()
                                for kc in range(KC):
                                    mm(ps_g[:, 0:N], wc[:, 1, kc, :], NT[:, kc, s:s + N], kc == 0, kc == KC - 1,
                                       (wcr, rNT(kc, si)), (pr_g,))
                                sg, sgr = TS.get()
                                kb.op("act", lambda e, sg=sg, ps_g=ps_g, N=N, c=c: e.activation(
                                    out=sg[:, 0:N], in_=ps_g[:, 0:N], func=AF.Exp, scale=-1.0, bias=NBC[:, c:c + 1]),
                                    (pr_g, rC), (sgr,))
                                yield
                                kb.op("act", lambda e, sg=sg, N=N: e.activation(
                                    out=sg[:, 0:N], in_=sg[:, 0:N], func=AF.Ln, bias=ONE1), (sgr, rC), (sgr,))
                                yield
                                kb.op("act", lambda e, sg=sg, N=N: e.activation(
                                    out=sg[:, 0:N], in_=sg[:, 0:N], func=AF.Exp, scale=-1.0), (sgr,), (sgr,))
                                yield
                                sgv = sg[:, 0:N] if isctx else sg[:, 0:N].rearrange("p (r c) -> p r c", c=64)
                                kb.op("dve", lambda e, interior=interior, psv=psv, ps_a=ps_a, sgv=sgv, c=c: e.scalar_tensor_tensor(
                                    out=interior, in0=psv(ps_a), scalar=vec(l, "in_b", 12 + c), in1=sgv,
                                    op0=ALU.add, op1=ALU.mult), (pr_a, sgr, rC), (kb.reg("VPs", si),))
                            for (s, N, isctx) in segs_loc:
                                si = seg_idx[(s, N, isctx)]
                                interior, tap, psv = padviews(VPc, VPl, s, N, isctx, 15)
                                ps, pr = bankC()
                                for k in range(31):
                                    mm(psv(ps), DGC[:, k, :], tap(k), k == 0, k == 30,
                                       (kb.reg("DGC", k), kb.reg("VPs", si), rvp), (pr,))
                                    if k % 4 == 3:
                                        yield
                                kb.op("dve", lambda e, ps=ps, s=s, N=N, c=c: e.tensor_scalar(
                                    out=YC[:, c, s:s + N], in0=ps[:, 0:N], scalar1=vec(l, "ccb", c), scalar2=None, op0=ALU.add),
                                    (pr, rC), (kb.reg("YC", c, si),))
                        segs_ln = [sg for sg in SEG256 if not (last and sg[2])]

                        def ln_gen(slot):
                            mean, mr = TL.t[3 * slot], kb.reg("cl", 3 * slot)
                            var, vr = TL.t[3 * slot + 1], kb.reg("cl", 3 * slot + 1)
                            t1, t1r = TL.t[3 * slot + 2], kb.reg("cl", 3 * slot + 2)
                            sq, sqr = SQ.t[slot], kb.reg("csq", slot)
                            ps1, pr1 = PS[4 + 2 * slot], kb.reg("ps", 4 + 2 * slot)
                            ps2, pr2 = PS[5 + 2 * slot], kb.reg("ps", 5 + 2 * slot)
                            for (s, N, isctx) in segs_ln[slot::2]:
                                s5 = 0 if isctx else 1 + (s - CTX) // 512
                                for c in range(4):
                                    mm(ps1[:, 0:N], ONES[:], YC[:, c, s:s + N], c == 0, c == 3, (kb.reg("YC", c, s5), rC), (pr1,))
                                yield
                                for c in range(4):
                                    kb.op("dve", lambda e: e.tensor_tensor(
                                        out=sq[:, 0:N], in0=YC[:, c, s:s + N], in1=YC[:, c, s:s + N], op=ALU.mult),
                                        (kb.reg("YC", c, s5),), (sqr,))
                                    mm(ps2[:, 0:N], ONES[:], sq[:, 0:N], c == 0, c == 3, (sqr, rC), (pr2,))
                                    yield
                                kb.op("act", lambda e: e.activation(
                                    out=mean[:, 0:N], in_=ps1[:, 0:N], func=AF.Identity, scale=1.0 / 512), (pr1,), (mr,))
                                yield
                                kb.op("dve", lambda e: e.tensor_tensor(
                                    out=var[:, 0:N], in0=mean[:, 0:N], in1=mean[:, 0:N], op=ALU.mult), (mr,), (vr,))
                                yield
                                kb.op("dve", lambda e: e.scalar_tensor_tensor(
                                    out=var[:, 0:N], in0=ps2[:, 0:N], scalar=1.0 / 512, in1=var[:, 0:N],
                                    op0=ALU.mult, op1=ALU.subtract), (pr2, vr), (vr,))
                                yield
                                kb.op("act", lambda e: e.activation(
                                    out=var[:, 0:N], in_=var[:, 0:N], func=AF.Ln, bias=EPS_LN), (vr, rC), (vr,))
                                yield
                                kb.op("act", lambda e: e.activation(
                                    out=var[:, 0:N], in_=var[:, 0:N], func=AF.Exp, scale=-0.5), (vr,), (vr,))
                                yield
                                for c in range(4):
                                    kb.op("dve", lambda e: e.tensor_tensor(
                                        out=t1[:, 0:N], in0=YC[:, c, s:s + N], in1=mean[:, 0:N], op=ALU.subtract),
                                        (kb.reg("YC", c, s5), mr), (t1r,))
                                    yield
                                    kb.op("dve", lambda e: e.tensor_tensor(
                                        out=t1[:, 0:N], in0=t1[:, 0:N], in1=var[:, 0:N], op=ALU.mult), (t1r, vr), (t1r,))
                                    yield
                                    kb.op("act", lambda e: e.activation(
                                        out=YC[:, c, s:s + N], in_=t1[:, 0:N], func=AF.Silu, scale=vec(l, "clg", c),
                                        bias=vec(l, "clb", c)), (t1r, rC), (kb.reg("YC", c, s5),))
                                    yield

                        yield from rr_gen([ln_gen(0), ln_gen(1)])

                    run_rr([conf_gen(), lru_gen()])
                    kb.phase_end(KEEP)

                hst = ExitStack()
                H = sb("H", [128, KC, T], F32, hst)
                if l == 0:
                    load_h_from_inputs()
                else:
                    for kc in range(KC):
                        kb.dma("sp", H[:, kc, :], hsp[kc], (kb.reg("hsp", kc),), tuple(rH(kc, i) for i in range(5)))

                with ExitStack() as ph:
                    WGJ = Rot(kb, ph, "WGJ", [128, 3, KC, 128], BF16, 2)
                    WOJ = Rot(kb, ph, "WOJ", [128, 3, 4, 128], BF16, 2)
                    MG = Rot(kb, ph, "MG", [128, T], BF16, 2)
                    WMX = Rot(kb, ph, "WMX", [128, 2, D], BF16, 1)
                    TG = Rot(kb, ph, "mt", [128, 512], F32, 3)
                    mixw = mix_w[l].rearrange("(j p) n -> p j n", p=128)
                    outs = [w_[l].rearrange("(kc p) n -> p kc n", p=128) for w_ in (fo_w, lo_w, co_w)]
                    YS = (YF, YR, YC)
                    YN = ("YF", "YR", "YC")
                    pair = []

                    def load_j(j):
                        wg, wgr = WGJ.get()
                        wo, wor = WOJ.get()
                        for k3 in range(3):
                            c0 = OFF_G + k3 * D + j * 128
                            kb.dma("pool", wg[:, k3, :, :], inw[:, :, c0:c0 + 128], (), (wgr,))
                            kb.dma("pool", wo[:, k3, :, :], outs[k3][:, :, j * 128:(j + 1) * 128], (), (wor,))
                        return wg, wgr, wo, wor

                    nxt = load_j(0)
                    for j in range(8):
                        wg, wgr, wo, wor = nxt
                        if j % 2 == 0:
                            wm, wmr = WMX.get()
                            kb.dma("pool", wm[:], mixw[:, j:j + 2, :], (), (wmr,))
                        if j < 7:
                            nxt = load_j(j + 1)
                        mg, mgr_base = MG.get()
                        pair.append((mg, mgr_base))
                        for (s, N, isctx) in segs_loc:
                            si = seg_idx[(s, N, isctx)]
                            tg = []
                            for k3 in range(3):
                                ps, pr = bank()
                                for kc in range(KC):
                                    mm(ps[:, 0:N], wg[:, k3, kc, :], NT[:, kc, s:s + N], kc == 0, kc == KC - 1,
                                       (wgr, rNT(kc, si)), (pr,))
                                t, tr_ = TG.get()
                                kb.op("act", lambda e, t=t, ps=ps, N=N, k3=k3, j=j: e.activation(
                                    out=t[:, 0:N], in_=ps[:, 0:N], func=AF.Sigmoid,
                                    bias=vec(l, "in_b", 20 + k3 * 8 + j)), (pr, rC), (tr_,))
                                ps2, pr2 = bank()
                                for c in range(4):
                                    yreg = kb.reg(YN[k3], c, si) if k3 != 0 else None
                                    rds = [wor]
                                    if k3 == 0:
                                        rds += [kb.reg("YF", c, i) for i in (range(0, 1) if isctx else range(1, 9))]
                                    elif k3 == 1:
                                        rds += [kb.reg("YR", c, i) for i in (range(0, 1) if isctx else range(1, 9))]
                                    else:
                                        rds.append(yreg)
                                    mm(ps2[:, 0:N], wo[:, k3, c, :], YS[k3][:, c, s:s + N], c == 0, c == 3, rds, (pr2,))
                                kb.op("dve", lambda e, t=t, ps2=ps2, N=N: e.tensor_tensor(
                                    out=t[:, 0:N], in0=t[:, 0:N], in1=ps2[:, 0:N], op=ALU.mult), (tr_, pr2), (tr_,))
                                tg.append((t, tr_))
                            kb.op("dve", lambda e, tg=tg, N=N: e.tensor_tensor(
                                out=tg[0][0][:, 0:N], in0=tg[0][0][:, 0:N], in1=tg[1][0][:, 0:N], op=ALU.add),
                                (tg[0][1], tg[1][1]), (tg[0][1],))
                            kb.op("dve", lambda e, tg=tg, mg=mg, s=s, N=N: e.tensor_tensor(
                                out=mg[:, s:s + N], in0=tg[0][0][:, 0:N], in1=tg[2][0][:, 0:N], op=ALU.add),
                                (tg[0][1], tg[2][1]), (kb.reg("MGs", j % 3, si), mgr_base))
                        if j % 2 == 1:
                            q = j // 2
                            for m in range(KC):
                                for (s, N, isctx) in segs_loc:
                                    si = seg_idx[(s, N, isctx)]
                                    r = 2 if isctx else bi
                                    ps, pr = bank()
                                    for jj in range(2):
                                        mm(ps[:, 0:N], wm[:, jj, m * 128:(m + 1) * 128], pair[jj][0][:, s:s + N],
                                           jj == 0, jj == 1, (wmr, pair[jj][1]), (pr,))
                                    kb.op("dve", lambda e, ps=ps, m=m, s=s, N=N, r=r: e.scalar_tensor_tensor(
                                        out=H[:, m, s:s + N], in0=ps[:, 0:N], scalar=mod(l, 2, m, r),
                                        in1=H[:, m, s:s + N], op0=ALU.mult, op1=ALU.add),
                                        (pr, rC, rC2, rH(m, si)), (rH(m, si),))
                            pair = []
                    kb.phase_end(KEEP)

                with ExitStack() as ph:
                    WU = ViewRot(kb, "WU", [BREG[:, 8 * T + i * 4096:8 * T + (i + 1) * 4096].rearrange(
                        "p (a k n) -> p a k n", a=2, k=KC) for i in range(2)])
                    WD = Rot(kb, ph, "WD", [128, 4, D], BF16, 1)
                    DGF = Rot(kb, ph, "DGF", [128, 2, 3, 128], BF16, 2)
                    UPc = sb("UPc", [128, 2, CTX + 2], BF16, ph)
                    UPl = sb("UPl", [128, 2, 32, 66], BF16, ph)
                    AC = ViewRot(kb, "AC", [YF, YR])
                    TF = Rot(kb, ph, "ft", [128, 512], F32, 2)
                    rup = kb.reg("UP")
                    kb.op("dve", lambda e: e.memset(UPc[:], 0.0), (), (rup,))
                    kb.op("dve", lambda e: e.memset(UPl[:], 0.0), (), (rup,))
                    with ExitStack() as phn:
                        tmp = (Rot(kb, phn, "sq", [128, 512], BF16, 3), Rot(kb, phn, "rs", [128, 512], F32, 3),
                               Rot(kb, phn, "tm", [128, 512], F32, 3))
                        rms_mod(l, 1, 3, bi, segs_loc, tmp, nt_out, nslots=3)
                    upw = up_w[l].rearrange("(kc p) n -> p kc n", p=128)
                    dnw = dn_w[l].rearrange("(j p) n -> p j n", p=128)
                    NJ = FH // 128
                    ac = acr = None
                    wu = wur = None
                    def load_wu(j):
                        wu_, wur_ = WU.get()
                        kb.dma("pool", wu_[:, 0, :, :], upw[:, :, j * 128:(j + 2) * 128], (), (wur_,))
                        kb.dma("pool", wu_[:, 1, :, :], upw[:, :, FH + j * 128:FH + (j + 2) * 128], (), (wur_,))
                        return wu_, wur_

                    nxt_wu = load_wu(0)
                    for j in range(NJ):
                        if j % 2 == 0:
                            wu, wur = nxt_wu
                            if j + 2 < NJ:
                                nxt_wu = load_wu(j + 2)
                        if j % 4 == 0:
                            ac, acr = AC.get()
                            g0 = j
                            ng = min(4, NJ - j)
                            wd, wdr = WD.get()
                            kb.dma("pool", wd[:, 0:ng, :], dnw[:, g0:g0 + ng, :], (), (wdr,))
                        jj = j % 2
                        dg, dgr = DGF.get()
                        for vg in range(2):
                            for k in range(3):
                                cc = vg * NJ + j
                                kb.op("dve", lambda e, dg=dg, vg=vg, k=k, cc=cc: e.tensor_scalar(
                                    out=dg[:, vg, k, :], in0=IDB[:], scalar1=vec(l, "fcw", cc * 3 + k), scalar2=None,
                                    op0=ALU.mult), (rC,), (dgr,))
                        def ffn_up(seg):
                            s, N, isctx = seg
                            si = seg_idx[seg]
                            for vg in range(2):
                                interior, tap, psv = padviews(UPc[:, vg, :], UPl[:, vg, :, :], s, N, isctx, 1)
                                ps, pr = bank()
                                for kc in range(KC):
                                    mm(ps[:, 0:N], wu[:, vg, kc, jj * 128:(jj + 1) * 128], NT[:, kc, s:s + N],
                                       kc == 0, kc == KC - 1, (wur, rNT(kc, si)), (pr,))
                                copy_any(interior, psv(ps), (pr,), (kb.reg("UPs", vg, si),))

                        def ffn_conv(seg):
                            s, N, isctx = seg
                            si = seg_idx[seg]
                            pcs = []
                            for vg in range(2):
                                interior, tap, psv = padviews(UPc[:, vg, :], UPl[:, vg, :, :], s, N, isctx, 1)
                                ps, pr = bank()
                                for k in range(3):
                                    mm(psv(ps), dg[:, vg, k, :], tap(k), k == 0, k == 2,
                                       (dgr, kb.reg("UPs", vg, si), rup), (pr,))
                                pcs.append((ps, pr))
                            t, tr_ = TF.get()
                            kb.op("act", lambda e: e.activation(
                                out=t[:, 0:N], in_=pcs[1][0][:, 0:N], func=AF.Silu, bias=vec(l, "fcb", NJ + j)),
                                (pcs[1][1], rC), (tr_,))
                            kb.op("dve", lambda e: e.scalar_tensor_tensor(
                                out=ac[:, j % 4, s:s + N], in0=pcs[0][0][:, 0:N], scalar=vec(l, "fcb", j),
                                in1=t[:, 0:N], op0=ALU.add, op1=ALU.mult), (pcs[0][1], tr_, rC),
                                (kb.reg("ACs", j % 4, si), acr))

                        for bi_, seg in enumerate(segs_loc):
                            ffn_up(seg)
                            if bi_ > 0:
                                ffn_conv(segs_loc[bi_ - 1])
                        ffn_conv(segs_loc[-1])
                        if j % 4 == 3 or j == NJ - 1:
                            for m in range(KC):
                                for (s, N, isctx) in segs_loc:
                                    si = seg_idx[(s, N, isctx)]
                                    r = 2 if isctx else bi
                                    ps, pr = bank()
                                    for q in range(ng):
                                        mm(ps[:, 0:N], wd[:, q, m * 128:(m + 1) * 128], ac[:, q, s:s + N],
                                           q == 0, q == ng - 1, (wdr, acr), (pr,))
                                    kb.op("dve", lambda e, ps=ps, m=m, s=s, N=N, r=r: e.scalar_tensor_tensor(
                                        out=H[:, m, s:s + N], in0=ps[:, 0:N], scalar=mod(l, 5, m, r),
                                        in1=H[:, m, s:s + N], op0=ALU.mult, op1=ALU.add),
                                        (pr, rC, rC2, rH(m, si)), (rH(m, si),))
                    kb.phase_end(KEEP)

            with ExitStack() as ph:
                tmp = (Rot(kb, ph, "sq", [128, 512], BF16, 3), Rot(kb, ph, "rs", [128, 512], F32, 2),
                       Rot(kb, ph, "tm", [128, 512], F32, 3))
                OUTB = Rot(kb, ph, "ob", [128, 512], F32, 4)
                lat = [sg for sg in SEG512 if not sg[2]]
                for sg in lat:
                    s, N, _ = sg
                    si = SEG512.index(sg)
                    ps, pr = bank()
                    SQ, RS, TM = tmp
                    for kc in range(KC):
                        sq, sqr = SQ.get()
                        kb.op("act", lambda e, sq=sq, kc=kc, s=s, N=N: e.activation(
                            out=sq[:, 0:N], in_=H[:, kc, s:s + N], func=AF.Square), (rH(kc, si),), (sqr,))
                        mm(ps[:, 0:N], ONES[:], sq[:, 0:N], kc == 0, kc == KC - 1, (sqr, rC), (pr,))
                    rs, rsr = RS.get()
                    kb.op("act", lambda e, rs=rs, ps=ps, N=N: e.activation(
                        out=rs[:, 0:N], in_=ps[:, 0:N], func=AF.Sqrt, scale=1.0 / D, bias=EPS_RMS), (pr, rC), (rsr,))
                    kb.op("dve", lambda e, rs=rs, N=N: e.reciprocal(out=rs[:, 0:N], in_=rs[:, 0:N]), (rsr,), (rsr,))
                    for kc in range(KC):
                        ob, obr = OUTB.get()
                        kb.op("dve", lambda e, ob=ob, rs=rs, kc=kc, s=s, N=N: e.scalar_tensor_tensor(
                            out=ob[:, 0:N], in0=H[:, kc, s:s + N], scalar=vec(0, "fing", kc), in1=rs[:, 0:N],
                            op0=ALU.mult, op1=ALU.mult), (rH(kc, si), rsr, rC), (obr,))
                        kb.dma("sp", outT[bi, kc * 128:(kc + 1) * 128, s - CTX:s - CTX + N], ob[:, 0:N], (obr,), ())
                kb.phase_end(KEEP)
            hst.close()
        kb.barrier()
    return nc


_CACHE = {}


def _fm(v, n):
    return np.ascontiguousarray(np.asarray(v, np.float32).reshape(n, 128).T)


def _fm_conv(w, n):
    K = w.shape[0]
    return np.ascontiguousarray(np.asarray(w, np.float32).reshape(K, n, 128).transpose(2, 1, 0).reshape(128, n * K))


def _dft_consts():
    if "dft" in _CACHE:
        return _CACHE["dft"]
    L = SEQ
    idx = (np.arange(L, dtype=np.int64)[:, None] * np.arange(L, dtype=np.int64)[None, :]) % L
    ang = 2.0 * np.pi * idx.astype(np.float64) / L
    sc = 1.0 / np.sqrt(L)
    mats = []
    mids = []
    for fn in (np.cos, np.sin):
        Mx = (fn(ang) * sc).astype(np.float32)
        mids.append(np.ascontiguousarray(Mx[:, L // 2:L // 2 + 2].reshape(16, 128, 2).transpose(1, 0, 2).reshape(128, 32)))
        Mx = Mx.reshape(16, 128, 8, 256).transpose(2, 1, 0, 3).reshape(8, 128, 16 * 256)[0:4]
        mats.append(np.ascontiguousarray(Mx))
    dftL = np.stack(mats)
    dftM = np.stack(mids)
    Lc = CTX
    idx = (np.arange(Lc)[:, None] * np.arange(Lc)[None, :]) % Lc
    ang = 2.0 * np.pi * idx / Lc
    sc = 1.0 / np.sqrt(Lc)
    mats = []
    for fn in (np.cos, np.sin):
        Mx = (fn(ang) * sc).astype(np.float32).reshape(2, 128, 256).transpose(1, 0, 2).reshape(128, 512)
        mats.append(np.ascontiguousarray(Mx))
    dftC = np.stack(mats)
    Kc = 128
    idx = (np.arange(Kc)[:, None] * np.arange(Kc)[None, :]) % Kc
    ang = 2.0 * np.pi * idx / Kc
    sc = 1.0 / np.sqrt(Kc)
    dftK = np.stack([(np.cos(ang) * sc).astype(np.float32), (-np.sin(ang) * sc).astype(np.float32),
                     (np.sin(ang) * sc).astype(np.float32)])
    _CACHE["dft"] = (dftL, dftC, dftK, dftM)
    return _CACHE["dft"]


def kernel(x, c, ctx, c_ctx, ada_w, ada_b, norm1_g, norm2_g, in_w, in_b, fourier_out_w,
           lru_conv_w, lru_conv_b, lru_wa, lru_ba, lru_wx, lru_bx, lru_lam, lru_out_w,
           conf_conv_w, conf_conv_b, conf_ln_g, conf_ln_b, conf_out_w, mix_out_w,
           ffn_up_w, ffn_conv_w, ffn_conv_b, ffn_down_w, final_g):
    f = lambda a: np.ascontiguousarray(np.asarray(a, np.float32))
    x = f(x); ctx = f(ctx); c = f(c); c_ctx = f(c_ctx)
    B = x.shape[0]
    vecs = np.zeros((128, DEPTH, NV), np.float32)
    for l in range(DEPTH):
        def put(name, arr):
            vecs[:, l, VEC[name]:VEC[name] + arr.shape[1]] = arr
        put("ada_b", _fm(ada_b[l], 48)); put("n1g", _fm(norm1_g[l], 8)); put("n2g", _fm(norm2_g[l], 8))
        put("in_b", _fm(in_b[l], 44)); put("lcw", _fm_conv(np.asarray(lru_conv_w[l]), 4)); put("lcb", _fm(lru_conv_b[l], 4))
        put("lba", _fm(np.asarray(lru_ba[l]).reshape(-1), 8)); put("lbx", _fm(np.asarray(lru_bx[l]).reshape(-1), 8))
        put("lam", _fm(np.asarray(lru_lam[l]).reshape(-1), 8))
        put("ccw", _fm_conv(np.asarray(conf_conv_w[l]), 4)); put("ccb", _fm(conf_conv_b[l], 4))
        put("clg", _fm(conf_ln_g[l], 4)); put("clb", _fm(conf_ln_b[l], 4))
        put("fcw", _fm_conv(np.asarray(ffn_conv_w[l]), 44)); put("fcb", _fm(ffn_conv_b[l], 44))
        put("fing", _fm(final_g, 8))
    inbf = f(np.asarray(in_b)[:, None, 0:512])
    dftL, dftC, dftK, dftM = _dft_consts()
    shared = {
        "vecs": vecs, "inbf": inbf, "ada_w": f(ada_w), "in_w": f(in_w), "fourier_out_w": f(fourier_out_w),
        "lru_out_w": f(lru_out_w), "conf_out_w": f(conf_out_w), "mix_out_w": f(mix_out_w),
        "ffn_up_w": f(ffn_up_w), "ffn_down_w": f(ffn_down_w), "lru_wa": f(lru_wa), "lru_wx": f(lru_wx),
        "dftL": dftL, "dftC": dftC, "dftK": dftK, "dftM": dftM, "ident": np.eye(128, dtype=np.float32),
    }
    xTa = np.ascontiguousarray(x.transpose(0, 2, 1))
    cTa = np.ascontiguousarray(ctx.transpose(0, 2, 1))
    in_maps = []
    for core in range(NCORES):
        b0 = core * BPC
        rows = np.stack([c[b0], c[b0 + 1], c_ctx])
        cTm = np.ascontiguousarray(rows.reshape(3, KC, 128).transpose(2, 1, 0))
        m = dict(shared)
        m["xT"] = xTa[b0:b0 + BPC]
        m["ctxT"] = cTa[b0:b0 + BPC]
        m["cT"] = cTm
        in_maps.append(m)
    if "nc" not in _CACHE:
        _CACHE["nc"] = build_program()
    res = run_bass_kernel_spmd(_CACHE["nc"], in_maps, core_ids=list(range(NCORES)))
    outT = np.concatenate([np.asarray(r["outT"]) for r in res.results], axis=0)
    return np.ascontiguousarray(outT.transpose(0, 2, 1)).astype(np.float32)
```
